# Optimizing a Trainium2 kernel written in Bass

```python
import math
import jax, jax.numpy as jnp
from jax import lax
import numpy as np

D_MODEL = 1024
BATCH = 8
SEQ = 8192
DEPTH = 2

N_ATT = (DEPTH + 1) // 2
N_LIN = DEPTH // 2

EPS = 1e-6
HEAD_DIM = 64
A_HEADS = 8
A_KV_HEADS = 2
A_GROUP = A_HEADS // A_KV_HEADS
WINDOW = 128
BLOCK = 128
B_HEADS = 8
B_NOPE = 64
B_ROPE = 32
B_VDIM = 64
B_Q_RANK = 384
B_KV_RANK = 256
ROPE_THETA = 10000.0
Q_BLOCK = 128
C_HEADS = 4
C_DK = 64
C_DV = 128
C_GATE_RANK = 16
C_GATE_NORM = 16.0
C_CHUNK = 64
D_HEADS = 4
D_DK = 64
D_DV = 128
D_CONV = 5
D_CHUNK = 64
D_FF = 4 * D_MODEL

A_Q = A_HEADS * HEAD_DIM
A_KV = A_KV_HEADS * HEAD_DIM
IN0_SPLITS = (A_Q, A_KV, A_KV, B_Q_RANK, B_KV_RANK, B_ROPE)
IN0 = sum(IN0_SPLITS)
MIX0 = A_HEADS * HEAD_DIM + B_HEADS * B_VDIM
B_QK = B_NOPE + B_ROPE

C_QK = C_HEADS * C_DK
C_V = C_HEADS * C_DV
D_QK = D_HEADS * D_DK
D_V = D_HEADS * D_DV
CONV_CH = 2 * D_QK + D_V
IN1_SPLITS = (C_QK, C_QK, C_V, C_V, C_GATE_RANK, C_GATE_RANK,
              CONV_CH, D_V, D_HEADS, D_HEADS, D_HEADS, D_HEADS)
IN1 = sum(IN1_SPLITS)
MIX1 = C_V + D_V

kernel_name = "hybrid_bidir_swa_mla_gla_gdn"

F32 = jnp.float32


def rms_norm(x, w):
    xf = x.astype(F32)
    y = xf * lax.rsqrt(jnp.mean(xf * xf, axis=-1, keepdims=True) + EPS)
    return (y * w.astype(F32)).astype(x.dtype)


def l2_norm(x):
    xf = x.astype(F32)
    return (xf * lax.rsqrt(jnp.sum(xf * xf, axis=-1, keepdims=True) + EPS)).astype(x.dtype)


def split_cols(t, sizes):
    idx = [int(i) for i in np.cumsum(sizes)[:-1]]
    return jnp.split(t, idx, axis=-1)


def flip(t):
    return jnp.flip(t, axis=1)


def alibi_slopes(n):
    return jnp.asarray(np.array([2.0 ** (-8.0 * (h + 1) / n) for h in range(n)], dtype=np.float32))


def rope(x):
    S = x.shape[1]
    half = x.shape[-1] // 2
    inv = ROPE_THETA ** (-jnp.arange(half, dtype=F32) / half)
    ang = jnp.arange(S, dtype=F32)[:, None] * inv[None, :]
    cos, sin = jnp.cos(ang)[:, None, :], jnp.sin(ang)[:, None, :]
    x1, x2 = x[..., :half].astype(F32), x[..., half:].astype(F32)
    return jnp.concatenate([x1 * cos - x2 * sin, x2 * cos + x1 * sin], axis=-1).astype(x.dtype)


def window_attention(q, k, v, sink):
    B, S = q.shape[0], q.shape[1]
    nb = S // BLOCK
    qb = q.reshape(B, nb, BLOCK, A_KV_HEADS, A_GROUP, HEAD_DIM)
    pad = ((0, 0), (BLOCK, BLOCK), (0, 0), (0, 0))
    kp = jnp.pad(k, pad).reshape(B, nb + 2, BLOCK, A_KV_HEADS, HEAD_DIM)
    vp = jnp.pad(v, pad).reshape(B, nb + 2, BLOCK, A_KV_HEADS, HEAD_DIM)
    kw = jnp.concatenate([kp[:, :-2], kp[:, 1:-1], kp[:, 2:]], axis=2)
    vw = jnp.concatenate([vp[:, :-2], vp[:, 1:-1], vp[:, 2:]], axis=2)
    s = jnp.einsum('bnqhgd,bnshd->bnhgqs', qb, kw, preferred_element_type=F32) * (HEAD_DIM ** -0.5)
    qi = jnp.arange(BLOCK)[:, None]
    kj = jnp.arange(3 * BLOCK)[None, :]
    dist = qi + BLOCK - kj
    kpos = jnp.arange(nb)[:, None] * BLOCK + jnp.arange(3 * BLOCK)[None, :] - BLOCK
    valid = (jnp.abs(dist) <= WINDOW)[None] & ((kpos >= 0) & (kpos < S))[:, None, :]
    slopes = alibi_slopes(A_HEADS).reshape(A_KV_HEADS, A_GROUP)
    s = s - slopes[:, :, None, None] * jnp.abs(dist).astype(F32)
    s = jnp.where(valid[None, :, None, None], s, -jnp.inf)
    sink_l = sink.astype(F32).reshape(A_KV_HEADS, A_GROUP)[None, None, :, :, None, None]
    m = jnp.maximum(jnp.max(s, axis=-1, keepdims=True), sink_l)
    p = jnp.exp(s - m)
    p = p / (jnp.sum(p, axis=-1, keepdims=True) + jnp.exp(sink_l - m))
    o = jnp.einsum('bnhgqs,bnshd->bnqhgd', p.astype(v.dtype), vw)
    return o.reshape(B, S, A_HEADS * HEAD_DIM)


def latent_attention(c_q, c_kv, k_pe, q_norm_w, w_uq, kv_norm_w, w_ukv):
    B, S = c_q.shape[0], c_q.shape[1]
    q = (rms_norm(c_q, q_norm_w) @ w_uq).reshape(B, S, B_HEADS, B_QK)
    q_nope, q_pe = q[..., :B_NOPE], rope(q[..., B_NOPE:])
    kv = (rms_norm(c_kv, kv_norm_w) @ w_ukv).reshape(B, S, B_HEADS, B_NOPE + B_VDIM)
    k_nope, v = kv[..., :B_NOPE], kv[..., B_NOPE:]
    k_r = rope(k_pe[:, :, None, :])[:, :, 0]
    nb = S // Q_BLOCK
    qn_b = jnp.moveaxis(q_nope.reshape(B, nb, Q_BLOCK, B_HEADS, B_NOPE), 1, 0)
    qp_b = jnp.moveaxis(q_pe.reshape(B, nb, Q_BLOCK, B_HEADS, B_ROPE), 1, 0)
    scale = B_QK ** -0.5

    def block(args):
        qn, qp = args
        s = (jnp.einsum('bqhd,bshd->bhqs', qn, k_nope, preferred_element_type=F32)
             + jnp.einsum('bqhr,bsr->bhqs', qp, k_r, preferred_element_type=F32)) * scale
        p = jax.nn.softmax(s, axis=-1)
        return jnp.einsum('bhqs,bshd->bqhd', p.astype(v.dtype), v)

    o = lax.map(block, (qn_b, qp_b))
    return jnp.moveaxis(o, 0, 1).reshape(B, S, B_HEADS * B_VDIM)


def gla_chunked(q, k, v, log_a):
    B, S, H, DK = q.shape
    DV = v.shape[-1]
    nc = S // C_CHUNK

    def chunks(t):
        return t.reshape(B, nc, C_CHUNK, H, t.shape[-1]).astype(F32)

    qc, kc, vc = chunks(q), chunks(k), chunks(v)
    b = jnp.cumsum(chunks(log_a), axis=2)
    b_last = b[:, :, -1:]
    q_in = qc * jnp.exp(b)
    k_in = kc * jnp.exp(-b)
    k_out = kc * jnp.exp(b_last - b)
    causal_in_chunk = jnp.tril(jnp.ones((C_CHUNK, C_CHUNK), dtype=bool))
    sc = jnp.where(causal_in_chunk, jnp.einsum('bnthd,bnshd->bnhts', q_in, k_in), 0.0)
    o_intra = jnp.einsum('bnhts,bnshv->bnthv', sc, vc)
    d_state = jnp.einsum('bnshd,bnshv->nbhdv', k_out, vc)
    decay = jnp.moveaxis(jnp.exp(b_last[:, :, 0]), 1, 0)

    def step(state, xs):
        dec, ds = xs
        return dec[..., None] * state + ds, state

    _, s_prev = lax.scan(step, jnp.zeros((B, H, DK, DV), F32), (decay, d_state))
    o_inter = jnp.einsum('bnthd,nbhdv->bnthv', q_in, s_prev)
    return (o_intra + o_inter).reshape(B, S, H, DV).astype(v.dtype)


def gla_mixer(q, k, v, g, gl_f, gl_b, w_gate_f, b_gate_f, w_gate_b, b_gate_b, norm_w):
    B, S = q.shape[0], q.shape[1]
    q = q.reshape(B, S, C_HEADS, C_DK) * (C_DK ** -0.5)
    k = k.reshape(B, S, C_HEADS, C_DK)
    v = v.reshape(B, S, C_HEADS, C_DV)
    la_f = (jax.nn.log_sigmoid((gl_f @ w_gate_f + b_gate_f).astype(F32)) / C_GATE_NORM).reshape(B, S, C_HEADS, C_DK)
    la_b = (jax.nn.log_sigmoid((gl_b @ w_gate_b + b_gate_b).astype(F32)) / C_GATE_NORM).reshape(B, S, C_HEADS, C_DK)
    o = gla_chunked(q, k, v, la_f) + flip(gla_chunked(flip(q), flip(k), flip(v), flip(la_b)))
    o = rms_norm(o, norm_w) * jax.nn.silu(g.reshape(B, S, C_HEADS, C_DV))
    return o.reshape(B, S, C_V)


def centred_conv(x, w):
    return lax.conv_general_dilated(
        x, w[:, None, :].astype(x.dtype), window_strides=(1,),
        padding=[(D_CONV // 2, D_CONV // 2)],
        dimension_numbers=('NWC', 'WIO', 'NWC'), feature_group_count=x.shape[-1])


def gated_delta_chunked(q, k, v, beta, g):
    B, S, H, DK = q.shape
    DV = v.shape[-1]
    C = D_CHUNK
    nc = S // C

    def chunks(t):
        return jnp.moveaxis(t.reshape(B, nc, C, H, t.shape[-1]).astype(F32), 3, 2)

    qc, kc, vc = chunks(q), chunks(k), chunks(v)
    bt = jnp.moveaxis(beta.reshape(B, nc, C, H).astype(F32), 3, 2)
    b = jnp.cumsum(jnp.moveaxis(g.reshape(B, nc, C, H).astype(F32), 3, 2), axis=-1)
    b_last = b[..., -1:]
    incl = jnp.tril(jnp.ones((C, C), dtype=bool))
    strict = jnp.tril(jnp.ones((C, C), dtype=bool), -1)
    gamma = jnp.exp(jnp.where(incl, b[..., :, None] - b[..., None, :], -jnp.inf))
    k_beta = kc * bt[..., None]
    lower = jnp.where(strict, jnp.einsum('bnhtd,bnhsd->bnhts', k_beta, kc) * gamma, 0.0)
    t_mat = lower + jnp.eye(C, dtype=F32)
    u = lax.linalg.triangular_solve(t_mat, vc * bt[..., None], left_side=True, lower=True, unit_diagonal=True)
    w = lax.linalg.triangular_solve(t_mat, k_beta * jnp.exp(b)[..., None], left_side=True, lower=True, unit_diagonal=True)
    attn = jnp.einsum('bnhtd,bnhsd->bnhts', qc, kc) * gamma
    q_dec = qc * jnp.exp(b)[..., None]
    k_dec = kc * jnp.exp(b_last - b)[..., None]
    decay = jnp.exp(b_last[..., 0])
    xs = tuple(jnp.moveaxis(t, 1, 0) for t in (u, w, attn, q_dec, k_dec, decay))

    def step(state, xs_c):
        u_c, w_c, attn_c, q_c, k_c, dec_c = xs_c
        v_new = u_c - jnp.einsum('bhcd,bhdv->bhcv', w_c, state)
        o = jnp.einsum('bhcd,bhdv->bhcv', q_c, state) + jnp.einsum('bhts,bhsv->bhtv', attn_c, v_new)
        state = dec_c[..., None, None] * state + jnp.einsum('bhcd,bhcv->bhdv', k_c, v_new)
        return state, o

    _, o = lax.scan(step, jnp.zeros((B, H, DK, DV), F32), xs)
    o = jnp.moveaxis(jnp.moveaxis(o, 0, 1), 2, 3)
    return o.reshape(B, S, H, DV).astype(v.dtype)


def delta_mixer(qkv, z, beta_f_in, beta_b_in, a_f_in, a_b_in, conv_w,
                a_log_f, dt_bias_f, a_log_b, dt_bias_b, norm_w):
    B, S = qkv.shape[0], qkv.shape[1]
    qkv = jax.nn.silu(centred_conv(qkv, conv_w))
    q, k, v = split_cols(qkv, (D_QK, D_QK, D_V))
    q = l2_norm(q.reshape(B, S, D_HEADS, D_DK)) * (D_DK ** -0.5)
    k = l2_norm(k.reshape(B, S, D_HEADS, D_DK))
    v = v.reshape(B, S, D_HEADS, D_DV)
    beta_f = jax.nn.sigmoid(beta_f_in.astype(F32))
    beta_b = jax.nn.sigmoid(beta_b_in.astype(F32))
    g_f = -jnp.exp(a_log_f.astype(F32)) * jax.nn.softplus(a_f_in.astype(F32) + dt_bias_f.astype(F32))
    g_b = -jnp.exp(a_log_b.astype(F32)) * jax.nn.softplus(a_b_in.astype(F32) + dt_bias_b.astype(F32))
    o = (gated_delta_chunked(q, k, v, beta_f, g_f)
         + flip(gated_delta_chunked(flip(q), flip(k), flip(v), flip(beta_b), flip(g_b))))
    o = rms_norm(o, norm_w) * jax.nn.silu(z.reshape(B, S, D_HEADS, D_DV))
    return o.reshape(B, S, D_V)


def setup_inputs(seed: int = 0) -> dict:
    key = jax.random.key(seed)
    ks = iter(jax.random.split(key, 48))

    def nrm(shape, fan_in):
        return jax.random.normal(next(ks), shape, F32) * (fan_in ** -0.5)

    def gain(shape):
        return 1.0 + 0.02 * jax.random.normal(next(ks), shape, F32)

    def small(shape, scale=0.01):
        return scale * jax.random.normal(next(ks), shape, F32)

    def a_log(shape):
        return jnp.log(jax.random.uniform(next(ks), shape, F32, minval=1.0, maxval=16.0))

    def dt_bias(shape):
        dt = jnp.exp(jax.random.uniform(next(ks), shape, F32, minval=math.log(1e-3), maxval=math.log(1e-1)))
        return dt + jnp.log(-jnp.expm1(-dt))

    return {
        "x": jax.random.normal(next(ks), (BATCH, SEQ, D_MODEL), F32),
        "att_norm": gain((N_ATT, D_MODEL)),
        "att_w_in": nrm((N_ATT, D_MODEL, IN0), D_MODEL),
        "att_sink": small((N_ATT, A_HEADS), 0.5),
        "mla_q_norm": gain((N_ATT, B_Q_RANK)),
        "mla_w_uq": nrm((N_ATT, B_Q_RANK, B_HEADS * B_QK), B_Q_RANK),
        "mla_kv_norm": gain((N_ATT, B_KV_RANK)),
        "mla_w_ukv": nrm((N_ATT, B_KV_RANK, B_HEADS * (B_NOPE + B_VDIM)), B_KV_RANK),
        "att_w_out": nrm((N_ATT, MIX0, D_MODEL), MIX0),
        "lin_norm": gain((N_LIN, D_MODEL)),
        "lin_w_in": nrm((N_LIN, D_MODEL, IN1), D_MODEL),
        "gla_w_gate_f": nrm((N_LIN, C_GATE_RANK, C_QK), C_GATE_RANK),
        "gla_b_gate_f": small((N_LIN, C_QK), 0.1),
        "gla_w_gate_b": nrm((N_LIN, C_GATE_RANK, C_QK), C_GATE_RANK),
        "gla_b_gate_b": small((N_LIN, C_QK), 0.1),
        "gla_norm": gain((N_LIN, C_DV)),
        "gdn_conv": nrm((N_LIN, D_CONV, CONV_CH), D_CONV),
        "gdn_a_log_f": a_log((N_LIN, D_HEADS)),
        "gdn_dt_bias_f": dt_bias((N_LIN, D_HEADS)),
        "gdn_a_log_b": a_log((N_LIN, D_HEADS)),
        "gdn_dt_bias_b": dt_bias((N_LIN, D_HEADS)),
        "gdn_norm": gain((N_LIN, D_DV)),
        "lin_w_out": nrm((N_LIN, MIX1, D_MODEL), MIX1),
        "mlp_norm": gain((DEPTH, D_MODEL)),
        "mlp_w1": nrm((DEPTH, D_MODEL, D_FF), D_MODEL),
        "mlp_w2": nrm((DEPTH, D_FF, D_MODEL), D_FF),
        "final_norm": gain((D_MODEL,)),
    }


def reference(x, att_norm, att_w_in, att_sink, mla_q_norm, mla_w_uq, mla_kv_norm, mla_w_ukv,
              att_w_out, lin_norm, lin_w_in, gla_w_gate_f, gla_b_gate_f, gla_w_gate_b, gla_b_gate_b,
              gla_norm, gdn_conv, gdn_a_log_f, gdn_dt_bias_f, gdn_a_log_b, gdn_dt_bias_b, gdn_norm,
              lin_w_out, mlp_norm, mlp_w1, mlp_w2, final_norm):
    B, S = x.shape[0], x.shape[1]
    for layer in range(DEPTH):
        i = layer // 2
        if layer % 2 == 0:
            h = rms_norm(x, att_norm[i])
            a_q, a_k, a_v, c_q, c_kv, k_pe = split_cols(h @ att_w_in[i], IN0_SPLITS)
            o_a = window_attention(a_q.reshape(B, S, A_HEADS, HEAD_DIM),
                                   a_k.reshape(B, S, A_KV_HEADS, HEAD_DIM),
                                   a_v.reshape(B, S, A_KV_HEADS, HEAD_DIM), att_sink[i])
            o_b = latent_attention(c_q, c_kv, k_pe, mla_q_norm[i], mla_w_uq[i],
                                   mla_kv_norm[i], mla_w_ukv[i])
            x = x + jnp.concatenate([o_a, o_b], axis=-1) @ att_w_out[i]
        else:
            h = rms_norm(x, lin_norm[i])
            (c_q, c_k, c_v, c_g, gl_f, gl_b, d_qkv, d_z,
             d_beta_f, d_beta_b, d_a_f, d_a_b) = split_cols(h @ lin_w_in[i], IN1_SPLITS)
            o_c = gla_mixer(c_q, c_k, c_v, c_g, gl_f, gl_b, gla_w_gate_f[i], gla_b_gate_f[i],
                            gla_w_gate_b[i], gla_b_gate_b[i], gla_norm[i])
            o_d = delta_mixer(d_qkv, d_z, d_beta_f, d_beta_b, d_a_f, d_a_b, gdn_conv[i],
                              gdn_a_log_f[i], gdn_dt_bias_f[i], gdn_a_log_b[i], gdn_dt_bias_b[i],
                              gdn_norm[i])
            x = x + jnp.concatenate([o_c, o_d], axis=-1) @ lin_w_out[i]
        m = rms_norm(x, mlp_norm[layer])
        x = x + jnp.square(jax.nn.relu(m @ mlp_w1[layer])) @ mlp_w2[layer]
    return rms_norm(x, final_norm)
```

```python
import numpy as np
from contextlib import ExitStack, contextmanager
import concourse.bass as bass
import concourse.mybir as mybir
from concourse.bass_utils import run_bass_kernel_spmd

F32 = mybir.dt.float32
BF16 = mybir.dt.bfloat16
F32R = mybir.dt.float32r
FR_GLA = F32R
FR_GDN = F32R
AF = mybir.ActivationFunctionType
ALU = mybir.AluOpType
AX = mybir.AxisListType

SAME_ENGINE_SYNC = True
SAME_ENGINE_WAR = True
N_DMA_SEMS = 72


class Buf:
    __slots__ = ("name", "t", "lastw", "readers", "sem")

    def __init__(self, name, t):
        self.name = name
        self.t = t
        self.lastw = None
        self.readers = []
        self.sem = None

    def __getitem__(self, k):
        return self.t[k]


class Prog:
    def __init__(self, nc):
        self.nc = nc
        self.es = ExitStack()
        self.es.enter_context(nc.allow_low_precision(reason="bf16 matmul operands, fp32 accumulation"))
        self.es.enter_context(nc.allow_non_contiguous_dma(reason="small one-time parameter loads"))
        self.eng = {"pe": nc.tensor, "act": nc.scalar, "dve": nc.vector,
                    "pool": nc.gpsimd, "sp": nc.sync}
        self.esem = {e: self.es.enter_context(nc.semaphore("es_" + e)) for e in self.eng}
        self.ecnt = {e: 0 for e in self.eng}
        self.dsems = [self.es.enter_context(nc.semaphore("ds%d" % i)) for i in range(N_DMA_SEMS)]
        self.dcnt = [0] * N_DMA_SEMS
        self.dfree = {True: list(range(0, 16)), False: list(range(16, N_DMA_SEMS))}
        self.waited = {e: {} for e in self.eng}
        self.phase_bufs = []
        self.pes = None
        self.n_instr = 0

    @contextmanager
    def phase(self, name):
        self.pes = ExitStack()
        self.phase_bufs = []
        self.pname = name
        try:
            yield
            self.barrier()
        finally:
            for b in self.phase_bufs:
                if b.sem is not None:
                    for sw, sidx in b.sem.items():
                        self.dfree[sw].append(sidx)
            self.pes.close()
            self.pes = None

    def sb(self, name, shape, dtype):
        t = self.pes.enter_context(self.nc.sbuf_tensor(self.pname + "_" + name, list(shape), dtype))
        b = Buf(name, t)
        self.phase_bufs.append(b)
        return b

    def ps(self, name, shape, dtype=F32):
        t = self.pes.enter_context(self.nc.psum_tensor(self.pname + "_" + name, list(shape), dtype))
        b = Buf(name, t)
        self.phase_bufs.append(b)
        return b

    def view(self, name, t):
        b = Buf(name, t)
        self.phase_bufs.append(b)
        return b

    def _deps(self, e, reads, writes):
        toks = []
        for b in reads:
            if b.lastw is not None:
                toks.append(b.lastw + (True,))
        for b in writes:
            if b.lastw is not None:
                toks.append(b.lastw + (False,))
            toks.extend(t + (False,) for t in b.readers)
        w = self.waited[e]
        eng = self.eng[e]
        for (kind, key, val, raw) in toks:
            if kind == "e":
                if key == e and (not SAME_ENGINE_SYNC or e in ("pe", "sp") or (not raw and not SAME_ENGINE_WAR)):
                    continue
                if w.get(("e", key), 0) >= val:
                    continue
                w[("e", key)] = val
                eng.wait_ge(self.esem[key], val)
            else:
                if w.get(("d", key), 0) >= val:
                    continue
                w[("d", key)] = val
                eng.wait_ge(self.dsems[key], val)

    def _mark(self, tok, reads, writes):
        for b in reads:
            b.readers.append(tok)
        for b in writes:
            b.lastw = tok
            b.readers = []

    def I(self, e, fn, reads=(), writes=(), signal=True):
        self._deps(e, reads, writes)
        ins = fn(self.eng[e])
        if signal:
            self.ecnt[e] += 1
            ins.then_inc(self.esem[e], 1)
            tok = ("e", e, self.ecnt[e])
        else:
            tok = ("e", e, self.ecnt[e] + 1)
        self._mark(tok, reads, writes)
        self.n_instr += 1
        return ins

    def dma(self, q, key, out, in_, reads=(), writes=(), **kw):
        sw = (q == "pool")
        if key.sem is None:
            key.sem = {}
        if sw not in key.sem:
            key.sem[sw] = self.dfree[sw].pop()
        s = key.sem[sw]
        self._deps(q, reads, writes)
        ins = self.eng[q].dma_start(out=out, in_=in_, **kw)
        self.dcnt[s] += 16
        ins.then_inc(self.dsems[s], 16)
        self._mark(("d", s, self.dcnt[s]), reads, writes)
        self.n_instr += 1
        return ins

    def barrier(self):
        sp = self.eng["sp"]
        w = self.waited["sp"]
        for f in self.eng:
            if f != "sp" and w.get(("e", f), 0) < self.ecnt[f]:
                w[("e", f)] = self.ecnt[f]
                sp.wait_ge(self.esem[f], self.ecnt[f])
        for s in range(N_DMA_SEMS):
            if self.dcnt[s] > 0 and w.get(("d", s), 0) < self.dcnt[s]:
                w[("d", s)] = self.dcnt[s]
                sp.wait_ge(self.dsems[s], self.dcnt[s])
        self.ecnt["sp"] += 1
        sp.sem_inc(self.esem["sp"], 1)
        v = self.ecnt["sp"]
        for f in self.eng:
            if f != "sp":
                self.waited[f][("e", "sp")] = v
                self.eng[f].wait_ge(self.esem["sp"], v)
                for g in self.eng:
                    self.waited[f][("e", g)] = self.ecnt[g]
                for s in range(N_DMA_SEMS):
                    self.waited[f][("d", s)] = self.dcnt[s]

    def finish(self):
        self.es.close()


D = 1024
DFF = 4096
EPS = 1e-6
IN0 = 1440
IN1 = 3120


def lam(f, *a, **k):
    return lambda e: getattr(e, f)(*a, **k)


class K:
    def __init__(self, nc, S):
        self.nc = nc
        self.S = S
        self.p = Prog(nc)
        self.din = {}
        self.scr = {}

    def inp(self, name, shape):
        self.din[name] = self.nc.dram_tensor(name, list(shape), F32, kind="ExternalInput").ap()
        return self.din[name]

    def scratch(self, name, shape, dt):
        self.scr[name] = self.nc.dram_tensor(name, list(shape), dt).ap()
        return self.scr[name]


def mm(p, outb, out_ap, pairs, reads):
    n = len(pairs)
    for i, (l, r) in enumerate(pairs):
        p.I("pe", lam("matmul", out_ap, l, r, start=(i == 0), stop=(i == n - 1)),
            reads=reads, writes=[outb], signal=(i == n - 1))


def ring(p, kind, name, n, shape, dtype=F32):
    f = p.sb if kind == "sb" else p.ps
    return [f("%s%d" % (name, i), shape, dtype) for i in range(n)]


def rmsnorm_fm(p, x, C, T, wn, ones, h, sq, ms, sd, rstd, Dn, sq_eng="pool"):
    for c in range(C):
        p.I(sq_eng, lam("tensor_tensor", sq[c][:], x[c][:], x[c][:], ALU.mult), reads=[x[c]], writes=[sq[c]])
    mm(p, ms, ms[:, 0:T], [(ones[:], sq[c][:]) for c in range(C)], reads=[ones] + sq[:C])
    p.I("act", lam("activation", sd[:, 0:T], ms[:, 0:T], AF.Sqrt, bias=EPS, scale=1.0 / Dn), reads=[ms], writes=[sd])
    p.I("dve", lam("reciprocal", rstd[:, 0:T], sd[:, 0:T]), reads=[sd], writes=[rstd])
    for c in range(C):
        p.I("dve", lam("scalar_tensor_tensor", h[c][:], x[c][:], wn[:, c:c + 1], rstd[:, 0:T], ALU.mult, ALU.mult),
            reads=[x[c], wn, rstd], writes=[h[c]])


def views(p, name, t, C):
    return [p.view("%s_%d" % (name, c), t[:, c]) for c in range(C)]


def phase_inproj0(k):
    p, S, W, SC = k.p, k.S, k.din, k.scr
    NG = S // 512
    with p.phase("inproj0"):
        win = p.sb("win", [128, 8, IN0], BF16)
        p.dma("pool", win, win[:], W["att_w_in"][0].rearrange("(c p) n -> p c n", p=128), writes=[win])
        wuq = p.sb("wuq", [128, 3, 768], BF16)
        p.dma("pool", wuq, wuq[:], W["mla_w_uq"][0].rearrange("(c p) n -> p c n", p=128), writes=[wuq])
        wukv = p.sb("wukv", [128, 2, 1024], BF16)
        p.dma("pool", wukv, wukv[:], W["mla_w_ukv"][0].rearrange("(c p) n -> p c n", p=128), writes=[wukv])
        wukv4 = wukv.t[:].rearrange("p c (h e) -> p c h e", e=128)
        wkn = p.sb("wkn", [128, 2, 512], BF16)
        wv = p.sb("wv", [128, 2, 512], BF16)
        for c in range(2):
            p.I("act", lam("copy", wkn[:, c, :].rearrange("p (h e) -> p h e", e=64), wukv4[:, c, :, 0:64]), reads=[wukv], writes=[wkn])
            p.I("act", lam("copy", wv[:, c, :].rearrange("p (h e) -> p h e", e=64), wukv4[:, c, :, 64:128]), reads=[wukv], writes=[wv])
        wksw = p.sb("wksw", [128, 8, 32], BF16)
        p.I("act", lam("mul", wksw[:, :, 0:16], win[:, :, 1424:1440], -1.0), reads=[win], writes=[wksw])
        p.I("act", lam("copy", wksw[:, :, 16:32], win[:, :, 1408:1424]), reads=[win], writes=[wksw])
        wuqsw = p.sb("wuqsw", [128, 3, 8, 32], BF16)
        wuq4 = wuq.t[:].rearrange("p c (h e) -> p c h e", e=96)
        for c in range(3):
            p.I("act", lam("mul", wuqsw[:, c, :, 0:16], wuq4[:, c, :, 80:96], -1.0), reads=[wuq], writes=[wuqsw])
            p.I("act", lam("copy", wuqsw[:, c, :, 16:32], wuq4[:, c, :, 64:80]), reads=[wuq], writes=[wuqsw])
        wn_att = p.sb("wn_att", [128, 8], F32)
        p.dma("sp", wn_att, wn_att[:], W["att_norm"][0].rearrange("(c p) -> p c", p=128), writes=[wn_att])
        wn_q = p.sb("wn_q", [128, 3], F32)
        p.dma("sp", wn_q, wn_q[:], W["mla_q_norm"][0].rearrange("(c p) -> p c", p=128), writes=[wn_q])
        wn_kv = p.sb("wn_kv", [128, 2], F32)
        p.dma("sp", wn_kv, wn_kv[:], W["mla_kv_norm"][0].rearrange("(c p) -> p c", p=128), writes=[wn_kv])
        ones = p.sb("ones", [128, 128], BF16)
        p.I("pool", lam("memset", ones[:], 1.0), writes=[ones])

        xt = ring(p, "sb", "xt", 2, [128, 8, 512], F32)
        xv = [views(p, "xv%d" % i, xt[i].t, 8) for i in range(2)]
        sqt = p.sb("sqt", [128, 8, 512], BF16)
        sq = views(p, "sq", sqt.t, 8)
        ht = p.sb("ht", [128, 8, 512], BF16)
        h = views(p, "h", ht.t, 8)
        sd = p.sb("sd", [128, 512], F32)
        rstd = p.sb("rstd", [128, 512], F32)
        cqt = p.sb("cqt", [128, 5, 512], F32)
        cq = views(p, "cq", cqt.t, 5)
        cqnt = p.sb("cqnt", [128, 5, 512], BF16)
        cqn = views(p, "cqn", cqnt.t, 5)
        csq = ring(p, "sb", "csq", 2, [128, 2, 512], F32)
        csk = ring(p, "sb", "csk", 2, [32, 2, 512], F32)
        stg = ring(p, "sb", "stg", 4, [128, 512], BF16)
        qst = ring(p, "sb", "qst", 3, [96, 512], BF16)
        t1 = ring(p, "sb", "t1", 2, [128, 512], F32)
        t2 = ring(p, "sb", "t2", 2, [128, 512], F32)
        kst = ring(p, "sb", "kst", 2, [32, 512], BF16)
        pp = ring(p, "ps", "pp", 7, [128, 512])
        ms = p.ps("ms", [128, 512])
        cnt = {"pp": 0, "stg": 0, "qst": 0, "t": 0}

        def nxt(nm, r):
            b = r[cnt[nm] % len(r)]
            cnt[nm] += 1
            return b

        xTv = W["xT"].rearrange("(c p) s -> p c s", p=128)
        rope = W["rope_cs"]
        MQ, MK, KPE, MV, AQKT, AV = SC["MQ"], SC["MK"], SC["KPE"], SC["MV"], SC["AQKT"], SC["AV"]
        MKf = MK.rearrange("h d s -> (h d) s")
        def load(g):
            gs = slice(g * 512, (g + 1) * 512)
            x = xv[g % 2]
            p.dma("sp", x[0], xt[g % 2][:, 0:4, :], xTv[:, 0:4, gs], writes=x[0:4])
            p.dma("act", x[4], xt[g % 2][:, 4:8, :], xTv[:, 4:8, gs], writes=x[4:8])
            cq_, ck_ = csq[g % 2], csk[g % 2]
            p.dma("sp", cq_, cq_[64:96, :, :], rope[:, :, gs].rearrange("t r s -> r t s"), writes=[cq_])
            p.dma("sp", ck_, ck_[:, :, :], rope[:, :, gs].rearrange("t r s -> r t s"), writes=[ck_])

        load(0)
        for g in range(NG):
            gs = slice(g * 512, (g + 1) * 512)
            x = xv[g % 2]
            cq_, ck_ = csq[g % 2], csk[g % 2]
            if g + 1 < NG:
                load(g + 1)
            rmsnorm_fm(p, x, 8, 512, wn_att, ones, h, sq, ms, sd, rstd, 1024)
            for t in range(11):
                if t == 5:
                    continue
                ps = nxt("pp", pp)
                mm(p, ps, ps[:], [(win[:, c, t * 128:(t + 1) * 128], h[c][:]) for c in range(8)], reads=[win] + h)
                if t < 5:
                    st = nxt("stg", stg)
                    p.I("act", lam("copy", st[:], ps[:]), reads=[ps], writes=[st])
                    p.dma("sp", st, AQKT[t * 128:(t + 1) * 128, gs], st[:], reads=[st])
                else:
                    p.I("act", lam("copy", cq[t - 6][:], ps[:]), reads=[ps], writes=[cq[t - 6]])
            ps = nxt("pp", pp)
            for j in range(4):
                mm(p, ps, ps[:, j * 128:(j + 1) * 128],
                   [(h[c][:, j * 128:(j + 1) * 128], win[:, c, 640:768]) for c in range(8)], reads=[win] + h)
            st = nxt("stg", stg)
            p.I("act", lam("copy", st[:], ps[:]), reads=[ps], writes=[st])
            p.dma("sp", st, AV[gs, :].rearrange("(j q) v -> q j v", q=128),
                  st[:].rearrange("q (j v) -> q j v", v=128), reads=[st])
            psk, pskw = nxt("pp", pp), nxt("pp", pp)
            mm(p, psk, psk[0:32, :], [(win[:, c, 1408:1440], h[c][:]) for c in range(8)], reads=[win] + h)
            mm(p, pskw, pskw[0:32, :], [(wksw[:, c, :], h[c][:]) for c in range(8)], reads=[wksw] + h)
            a1, a2 = nxt("t", t1), t2[(cnt["t"] - 1) % 2]
            p.I("dve", lam("tensor_tensor", a1[0:32, :], psk[0:32, :], ck_[:, 0, :], ALU.mult), reads=[psk, ck_], writes=[a1])
            p.I("dve", lam("tensor_tensor", a2[0:32, :], pskw[0:32, :], ck_[:, 1, :], ALU.mult), reads=[pskw, ck_], writes=[a2])
            ks = kst[g % 2]
            p.I("pool", lam("tensor_tensor", ks[:], a1[0:32, :], a2[0:32, :], ALU.add), reads=[a1, a2], writes=[ks])
            p.dma("sp", ks, KPE[:, gs], ks[:], reads=[ks])
            rmsnorm_fm(p, cq[0:3], 3, 512, wn_q, ones, cqn[0:3], sq[0:3], ms, sd, rstd, 384)
            rmsnorm_fm(p, cq[3:5], 2, 512, wn_kv, ones, cqn[3:5], sq[0:2], ms, sd, rstd, 256)
            for hh in range(8):
                pa, pb = nxt("pp", pp), nxt("pp", pp)
                mm(p, pa, pa[0:96, :], [(wuq[:, c, hh * 96:(hh + 1) * 96], cqn[c][:]) for c in range(3)], reads=[wuq] + cqn[0:3])
                mm(p, pb, pb[64:96, :], [(wuqsw[:, c, hh, :], cqn[c][:]) for c in range(3)], reads=[wuqsw] + cqn[0:3])
                qs = nxt("qst", qst)
                a1, a2 = nxt("t", t1), t2[(cnt["t"] - 1) % 2]
                p.I("act", lam("copy", qs[0:64, :], pa[0:64, :]), reads=[pa], writes=[qs])
                p.I("dve", lam("tensor_tensor", a1[64:96, :], pa[64:96, :], cq_[64:96, 0, :], ALU.mult), reads=[pa, cq_], writes=[a1])
                p.I("dve", lam("tensor_tensor", a2[64:96, :], pb[64:96, :], cq_[64:96, 1, :], ALU.mult), reads=[pb, cq_], writes=[a2])
                p.I("pool", lam("tensor_tensor", qs[64:96, :], a1[64:96, :], a2[64:96, :], ALU.add), reads=[a1, a2], writes=[qs])
                p.dma("sp", qs, MQ[hh, :, gs], qs[:], reads=[qs])
            for j in range(4):
                ps = nxt("pp", pp)
                mm(p, ps, ps[:], [(wkn[:, c, j * 128:(j + 1) * 128], cqn[3 + c][:]) for c in range(2)], reads=[wkn] + cqn[3:5])
                st = nxt("stg", stg)
                p.I("act", lam("copy", st[:], ps[:]), reads=[ps], writes=[st])
                p.dma("sp", st, MKf[j * 128:(j + 1) * 128, gs], st[:], reads=[st])
            for jt in range(4):
                ps = nxt("pp", pp)
                mm(p, ps, ps[:], [(cqn[3 + c][:, jt * 128:(jt + 1) * 128], wv[:, c, :]) for c in range(2)],
                   reads=[wv] + cqn[3:5])
                st = nxt("stg", stg)
                p.I("act", lam("copy", st[:], ps[:]), reads=[ps], writes=[st])
                p.dma("sp", st, MV[g * 512 + jt * 128:g * 512 + (jt + 1) * 128, :], st[:], reads=[st])


INPUT_SHAPES = {
    "att_norm": [1, 1024], "att_w_in": [1, 1024, 1440], "att_sink": [1, 8], "mla_q_norm": [1, 384],
    "mla_w_uq": [1, 384, 768], "mla_kv_norm": [1, 256], "mla_w_ukv": [1, 256, 1024], "att_w_out": [1, 1024, 1024],
    "lin_norm": [1, 1024], "lin_w_in": [1, 1024, 3120], "gla_w_gate_f": [1, 16, 256], "gla_b_gate_f": [1, 256],
    "gla_w_gate_b": [1, 16, 256], "gla_b_gate_b": [1, 256], "gla_norm": [1, 128], "gdn_conv": [1, 5, 1024],
    "gdn_a_log_f": [1, 4], "gdn_dt_bias_f": [1, 4], "gdn_a_log_b": [1, 4], "gdn_dt_bias_b": [1, 4],
    "gdn_norm": [1, 128], "lin_w_out": [1, 1024, 1024], "mlp_norm": [2, 1024], "mlp_w1": [2, 1024, 4096],
    "mlp_w2": [2, 4096, 1024], "final_norm": [1024],
}


def build(S, phases=("inproj0",), debug_outs=()):
    nc = bass.Bass("TRN2", target_bir_lowering=False)
    k = K(nc, S)
    k.inp("xT", [1024, S])
    for n, sh in INPUT_SHAPES.items():
        k.inp(n, sh)
    k.inp("rope_cs", [2, 32, S])
    outT = nc.dram_tensor("outT", [1024, S], F32, kind="ExternalOutput").ap()
    k.outT = outT

    def scr(name, shape, dt):
        if name in debug_outs:
            k.scr[name] = nc.dram_tensor(name, list(shape), dt, kind="ExternalOutput").ap()
        else:
            k.scratch(name, shape, dt)

    scr("AQKT", [640, S], BF16)
    scr("AV", [S, 128], BF16)
    scr("MQ", [8, 96, S], BF16)
    scr("MK", [8, 64, S], BF16)
    scr("KPE", [32, S], BF16)
    scr("MV", [S, 512], BF16)
    scr("OMIX", [1024, S], BF16)
    scr("XS", [1024, S], F32)
    scr("XS1", [1024, S], F32)
    k.inp("win_mask", [128, 8, 384])
    scr("CQT", [256, S], BF16)
    scr("CKT", [256, S], BF16)
    scr("CK", [S, 256], BF16)
    scr("CV", [S, 512], BF16)
    scr("SG", [512, S], BF16)
    scr("SZ", [512, S], BF16)
    scr("LA", [2, S, 256], F32)
    scr("DRAW", [1024, S], F32)
    scr("GB", [S, 16], F32)
    scr("OGF", [512, S], F32)
    scr("ODF", [512, S], F32)
    scr("DQT", [256, S], BF16)
    scr("DKT", [256, S], BF16)
    scr("DK", [S, 256], BF16)
    scr("DVT", [S, 512], BF16)
    k.inp("cmat", [128, 10, 128])
    if "inproj0" in phases:
        phase_inproj0(k)
    if "winattn" in phases:
        phase_winattn(k)
    if "mla" in phases:
        phase_mla(k)
    if "out0" in phases:
        phase_outproj(k, "out0", k.din["att_w_out"][0], k.din["xT"], k.scr["XS1"])
    if "mlp0" in phases:
        phase_mlp(k, 0, k.scr["XS1"], k.scr["XS"], False)
    if "inproj1" in phases:
        phase_inproj1(k, k.scr["XS"] if "mlp0" in phases else k.din["xT"])
    if "gdnprep" in phases:
        phase_gdnprep(k)
    if "gla" in phases:
        phase_gla(k)
    if "gdn" in phases:
        phase_gdn(k)
    if "out1" in phases:
        phase_outproj(k, "out1", k.din["lin_w_out"][0], k.scr["XS"] if "mlp0" in phases else k.din["xT"], k.scr["XS1"])
    if "mlp1" in phases:
        phase_mlp(k, 1, k.scr["XS1"], k.outT, True)
    print("[kernel] instr=%d eng counts=%s max dma sem=%d" % (k.p.n_instr, k.p.ecnt, max(k.p.dcnt)), flush=True)
    k.p.finish()
    return nc


def host_consts(S):
    half = 16
    inv = (10000.0 ** (-np.arange(half, dtype=np.float32) / half)).astype(np.float32)
    ang = np.arange(S, dtype=np.float32)[:, None] * inv[None, :]
    cos, sin = np.cos(ang).astype(np.float32).T, np.sin(ang).astype(np.float32).T
    rope_cs = np.stack([np.concatenate([cos, cos], 0), np.concatenate([sin, sin], 0)], 0)
    s_ = np.arange(128)[:, None, None]
    j_ = np.arange(3)[None, :, None]
    t_ = np.arange(128)[None, None, :]
    dist = np.abs(t_ - s_ - (j_ - 1) * 128).astype(np.float32)
    slopes = np.array([2.0 ** (-8.0 * (h + 1) / 8) for h in range(8)], dtype=np.float32)
    wm = np.exp(-slopes[None, :, None, None] * dist[:, None]) * (dist[:, None] <= 128)
    i_ = np.arange(128)[:, None]
    jj = np.arange(128)[None, :]
    LE, GE, LT, GT = (i_ <= jj), (i_ >= jj), (i_ < jj), (i_ > jj)
    blk = (i_ // 64) == (jj // 64)
    blk16 = (i_ // 16) == (jj // 16)
    blk32 = (i_ // 32) == (jj // 32)
    cm = np.stack([LE, GE, LT, GT, np.where(LE, 0.0, -30000.0), np.where(GE, 0.0, -30000.0), i_ == jj, blk, blk16, blk32], 1)
    return {"cmat": np.ascontiguousarray(cm, dtype=np.float32),
            "rope_cs": np.ascontiguousarray(rope_cs, dtype=np.float32),
            "win_mask": np.ascontiguousarray(wm.reshape(128, 8, 384), dtype=np.float32)}


def phase_winattn(k):
    p, S, W, SC = k.p, k.S, k.din, k.scr
    NB = S // 128
    AQKT, AV, OMIX = SC["AQKT"], SC["AV"], SC["OMIX"]
    with p.phase("winattn"):
        mask = p.sb("mask", [128, 8, 384], F32)
        p.dma("sp", mask, mask[:], W["win_mask"], writes=[mask])
        sink = p.sb("sink", [64, 8], F32)
        p.dma("sp", sink, sink[:], W["att_sink"][0].partition_broadcast(64), writes=[sink])
        es = p.sb("es", [64, 8], F32)
        p.I("act", lam("activation", es[:], sink[:], AF.Exp), reads=[sink], writes=[es])
        ones64 = p.sb("ones64", [128, 64], BF16)
        p.I("pool", lam("memset", ones64[:], 1.0), writes=[ones64])
        kT = ring(p, "sb", "kT", 2, [64, S], BF16)
        Vt = ring(p, "sb", "Vt", 2, [128, NB, 64], BF16)
        qT = ring(p, "sb", "qT", 2, [64, S], BF16)
        Eb = ring(p, "sb", "Eb", 3, [128, 384], BF16)
        Pb = ring(p, "sb", "Pb", 3, [128, 384], BF16)
        den = ring(p, "sb", "den", 2, [64, 512], F32)
        rinv = ring(p, "sb", "rinv", 2, [64, 512], F32)
        ob = ring(p, "sb", "ob", 2, [64, 512], BF16)
        pss = ring(p, "ps", "pss", 3, [128, 384])
        po = ring(p, "ps", "po", 2, [64, 512])
        pr = ring(p, "ps", "pr", 2, [64, 512])
        LOOK = 2
        its = [(gk, hl, n) for gk in range(2) for hl in range(4) for n in range(NB)]

        def load_kv(gk):
            kt, vt = kT[gk % 2], Vt[gk % 2]
            p.dma("sp", kt, kt[:], AQKT[512 + gk * 64:512 + (gk + 1) * 64, :], writes=[kt])
            p.dma("sp", vt, vt[:], AV[:, gk * 64:(gk + 1) * 64].rearrange("(n q) v -> q n v", q=128), writes=[vt])

        def load_q(hh):
            qt = qT[hh % 2]
            p.dma("act", qt, qt[:], AQKT[hh * 64:(hh + 1) * 64, :], writes=[qt])

        def scores(i):
            gk, hl, n = its[i]
            hh = gk * 4 + hl
            kt, qt = kT[gk % 2], qT[hh % 2]
            s_ = pss[i % 3]
            for j in range(3):
                if 0 <= n - 1 + j < NB:
                    mm(p, s_, s_[:, j * 128:(j + 1) * 128],
                       [(kt[:, (n - 1 + j) * 128:(n + j) * 128], qt[:, n * 128:(n + 1) * 128])], reads=[kt, qt])

        pending = []

        def finalize(hh, nq, i2):
            o_, r_ = po[i2], pr[i2]
            p.I("dve", lam("tensor_scalar", den[i2][:], r_[:], es[:, hh:hh + 1], None, ALU.add), reads=[r_, es], writes=[den[i2]])
            p.I("dve", lam("reciprocal", rinv[i2][:], den[i2][:]), reads=[den[i2]], writes=[rinv[i2]])
            p.I("dve", lam("tensor_tensor", ob[i2][:], o_[:], rinv[i2][:], ALU.mult), reads=[o_, rinv[i2]], writes=[ob[i2]])
            p.dma("sp", ob[i2], OMIX[hh * 64:(hh + 1) * 64, nq * 128:(nq + 4) * 128], ob[i2][:], reads=[ob[i2]])

        load_kv(0)
        load_q(0)
        for i in range(min(LOOK, len(its))):
            scores(i)
        for i, (gk, hl, n) in enumerate(its):
            hh = gk * 4 + hl
            if n == 0:
                if hl == 0 and gk + 1 < 2:
                    load_kv(gk + 1)
                if hh + 1 < 8:
                    load_q(hh + 1)
            if i + LOOK < len(its):
                scores(i + LOOK)
            vt = Vt[gk % 2]
            nq = (n // 4) * 4
            i2 = (n // 4) % 2
            o_, r_ = po[i2], pr[i2]
            js = [j for j in range(3) if 0 <= n - 1 + j < NB]
            s_, e_, p_ = pss[i % 3], Eb[i % 3], Pb[i % 3]
            c0, c1 = js[0] * 128, (js[-1] + 1) * 128
            p.I("act", lam("activation", e_[:, c0:c1], s_[:, c0:c1], AF.Exp, scale=0.125), reads=[s_], writes=[e_])
            p.I("pool", lam("tensor_tensor", p_[:, c0:c1], e_[:, c0:c1], mask[:, hh, c0:c1], ALU.mult),
                reads=[e_, mask], writes=[p_])
            cs = slice((n - nq) * 128, (n - nq + 1) * 128)
            mm(p, o_, o_[:, cs], [(vt[:, n - 1 + j, :], p_[:, j * 128:(j + 1) * 128]) for j in js], reads=[vt, p_])
            mm(p, r_, r_[:, cs], [(ones64[:], p_[:, j * 128:(j + 1) * 128]) for j in js], reads=[ones64, p_])
            if n % 4 == 1 and pending:
                finalize(*pending.pop(0))
            if n % 4 == 3:
                pending.append((hh, nq, i2))
        while pending:
            finalize(*pending.pop(0))


def phase_mla(k):
    p, S, W, SC = k.p, k.S, k.din, k.scr
    NB, NG = S // 128, S // 512
    NP = NB // 2
    MQ, MK, KPE, MV, OMIX = SC["MQ"], SC["MK"], SC["KPE"], SC["MV"], SC["OMIX"]
    scale = 96.0 ** -0.5
    LOOK = 2
    with p.phase("mla"):
        sel = p.sb("sel", [65, 64], F32)
        p.I("pool", lam("memset", sel[:], 0.0), writes=[sel])
        p.I("pool", lam("memset", sel[64:65, :], 1.0), writes=[sel])
        qT = ring(p, "sb", "mqT", 2, [96, S], BF16)
        kT = ring(p, "sb", "mkT", 2, [96, S], BF16)
        Vt = ring(p, "sb", "mV", 2, [128, NB, 65], BF16)
        for v in Vt:
            p.I("pool", lam("memset", v[:, :, 64:65], 1.0), writes=[v])
        Eb = ring(p, "sb", "mE", 3, [128, 2, 512], BF16)
        osb = ring(p, "sb", "osb", 2, [65, 512], F32)
        rinv = ring(p, "sb", "mrinv", 2, [64, 512], F32)
        ob = ring(p, "sb", "mob", 2, [64, 512], BF16)
        pss = ring(p, "ps", "mps", 3, [128, 2, 512])
        po = p.ps("mpo", [65, 512])
        pr = p.ps("mpr", [64, 512])

        def load(hh):
            qt, kt, vt = qT[hh % 2], kT[hh % 2], Vt[hh % 2]
            p.dma("sp", qt, qt[:], MQ[hh], writes=[qt])
            p.dma("sp", kt, kt[0:64, :], MK[hh], writes=[kt])
            p.dma("act", kt, kt[64:96, :], KPE, writes=[kt])
            p.dma("sp", vt, vt[:, :, 0:64], MV[:, hh * 64:(hh + 1) * 64].rearrange("(n q) v -> q n v", q=128), writes=[vt])

        its = [(hh, g, mp) for hh in range(8) for g in range(NG) for mp in range(NP)]

        def scores(i):
            hh, g, mp = its[i]
            qt, kt = qT[hh % 2], kT[hh % 2]
            s_ = pss[i % 3]
            for j in range(2):
                m = 2 * mp + j
                mm(p, s_, s_[:, j, :], [(kt[:, m * 128:(m + 1) * 128], qt[:, g * 512:(g + 1) * 512])], reads=[kt, qt])

        pending = []

        def finalize(hh, g):
            i2 = g % 2
            mm(p, pr, pr[:], [(sel[:], osb[i2][:])], reads=[sel, osb[i2]])
            p.I("dve", lam("reciprocal", rinv[i2][:], pr[:]), reads=[pr], writes=[rinv[i2]])
            p.I("dve", lam("tensor_tensor", ob[i2][:], osb[i2][0:64, :], rinv[i2][:], ALU.mult),
                reads=[osb[i2], rinv[i2]], writes=[ob[i2]])
            p.dma("sp", ob[i2], OMIX[512 + hh * 64:512 + (hh + 1) * 64, g * 512:(g + 1) * 512], ob[i2][:], reads=[ob[i2]])

        load(0)
        for i in range(min(LOOK, len(its))):
            scores(i)
        for i, (hh, g, mp) in enumerate(its):
            if g == 0 and mp == 0 and hh + 1 < 8:
                load(hh + 1)
            if i + LOOK < len(its):
                scores(i + LOOK)
            vt = Vt[hh % 2]
            s_, e_ = pss[i % 3], Eb[i % 3]
            p.I("act", lam("activation", e_[:], s_[:], AF.Exp, scale=scale), reads=[s_], writes=[e_])
            for j in range(2):
                m = 2 * mp + j
                p.I("pe", lam("matmul", po[:], vt[:, m, :], e_[:, j, :], start=(m == 0), stop=(m == NB - 1)),
                    reads=[vt, e_], writes=[po], signal=(j == 1))
            if mp == 3 and pending:
                finalize(*pending.pop(0))
            if mp == NP - 1:
                p.I("dve", lam("tensor_copy", osb[g % 2][:], po[:]), reads=[po], writes=[osb[g % 2]])
                pending.append((hh, g))
        while pending:
            finalize(*pending.pop(0))


def phase_outproj(k, name, wo_ap, xin, xout):
    p, S, W, SC = k.p, k.S, k.din, k.scr
    NG = S // 512
    OMIX = SC["OMIX"]
    with p.phase(name):
        wo = p.sb("wo", [128, 8, 1024], BF16)
        p.dma("pool", wo, wo[:], wo_ap.rearrange("(c p) n -> p c n", p=128), writes=[wo])
        xt = ring(p, "sb", "oxt", 2, [128, 8, 512], F32)
        xv = [views(p, "oxv%d" % i, xt[i].t, 8) for i in range(2)]
        om = ring(p, "sb", "om", 2, [128, 8, 512], BF16)
        pp = ring(p, "ps", "opp", 4, [128, 512])
        xiv = xin.rearrange("(c p) s -> p c s", p=128)
        xov = xout.rearrange("(c p) s -> p c s", p=128)
        omv = OMIX.rearrange("(c p) s -> p c s", p=128)

        def load(g):
            gs = slice(g * 512, (g + 1) * 512)
            p.dma("sp", xv[g % 2][0], xt[g % 2][:], xiv[:, :, gs], writes=xv[g % 2], reads=[])
            p.dma("act", om[g % 2], om[g % 2][:], omv[:, :, gs], writes=[om[g % 2]])

        load(0)
        for g in range(NG):
            gs = slice(g * 512, (g + 1) * 512)
            if g + 1 < NG:
                load(g + 1)
            x, o = xv[g % 2], om[g % 2]
            for c in range(8):
                ps = pp[(g * 8 + c) % 4]
                mm(p, ps, ps[:], [(wo[:, cc, c * 128:(c + 1) * 128], o[:, cc, :]) for cc in range(8)], reads=[wo, o])
                p.I("dve", lam("tensor_tensor", x[c][:], ps[:], x[c][:], ALU.add), reads=[ps, x[c]], writes=[x[c]])
            p.dma("sp", xv[g % 2][0], xov[:, :, gs], xt[g % 2][:], reads=x)


def phase_mlp(k, layer, xin, xout, final):
    p, S, W, SC = k.p, k.S, k.din, k.scr
    TG = 512
    NG = S // TG
    with p.phase("mlp%d" % layer):
        w1 = p.sb("w1", [128, 8, DFF], BF16)
        w1v = W["mlp_w1"][layer].rearrange("(c p) n -> p c n", p=128)
        for c in range(8):
            p.dma("pool", w1, w1[:, c, :], w1v[:, c, :], writes=[w1])
        w2 = p.sb("w2", [128, 32, D], BF16)
        w2v = W["mlp_w2"][layer].rearrange("(c p) n -> p c n", p=128)
        for c in range(0, 32, 4):
            p.dma("pool", w2, w2[:, c:c + 4, :], w2v[:, c:c + 4, :], writes=[w2])
        wn = p.sb("mwn", [128, 8], F32)
        p.dma("sp", wn, wn[:], W["mlp_norm"][layer].rearrange("(c p) -> p c", p=128), writes=[wn])
        if final:
            wf = p.sb("mwf", [128, 8], F32)
            p.dma("sp", wf, wf[:], W["final_norm"].rearrange("(c p) -> p c", p=128), writes=[wf])
        ones = p.sb("mones", [128, 128], BF16)
        p.I("pool", lam("memset", ones[:], 1.0), writes=[ones])
        xt = ring(p, "sb", "mxt", 2, [128, 8, TG], F32)
        xv = [views(p, "mxv%d" % i, xt[i].t, 8) for i in range(2)]
        mt = p.sb("mmt", [128, 8, TG], BF16)
        m = views(p, "mm_", mt.t, 8)
        at = p.sb("mat", [128, 32, TG], BF16)
        a = views(p, "ma", at.t, 32)
        sd = p.sb("msd", [128, TG], F32)
        rstd = p.sb("mrstd", [128, TG], F32)
        rr = ring(p, "sb", "mrr", 3, [128, TG], BF16)
        pp = ring(p, "ps", "mpp", 7, [128, TG])
        ms = p.ps("mms", [128, TG])
        xiv = xin.rearrange("(c p) s -> p c s", p=128)
        xov = xout.rearrange("(c p) s -> p c s", p=128)

        def load(g):
            gs = slice(g * TG, (g + 1) * TG)
            p.dma("sp", xv[g % 2][0], xt[g % 2][:, 0:4, :], xiv[:, 0:4, gs], writes=xv[g % 2][0:4])
            p.dma("act", xv[g % 2][4], xt[g % 2][:, 4:8, :], xiv[:, 4:8, gs], writes=xv[g % 2][4:8])

        load(0)
        it = 0
        for g in range(NG):
            gs = slice(g * TG, (g + 1) * TG)
            if g + 1 < NG:
                load(g + 1)
            x = xv[g % 2]
            rmsnorm_fm(p, x, 8, TG, wn, ones, m, m, ms, sd, rstd, 1024)
            for f in range(32):
                ps = pp[it % 7]
                it += 1
                mm(p, ps, ps[:], [(w1[:, c, f * 128:(f + 1) * 128], m[c][:]) for c in range(8)], reads=[w1] + m)
                r_ = rr[it % 3]
                p.I("act", lam("activation", r_[:], ps[:], AF.Relu), reads=[ps], writes=[r_])
                p.I("dve", lam("tensor_tensor", a[f][:], ps[:], r_[:], ALU.mult), reads=[ps, r_], writes=[a[f]])
            for c in range(8):
                ps = pp[it % 7]
                it += 1
                mm(p, ps, ps[:], [(w2[:, f, c * 128:(c + 1) * 128], a[f][:]) for f in range(32)], reads=[w2] + a)
                p.I("dve", lam("tensor_tensor", x[c][:], ps[:], x[c][:], ALU.add), reads=[ps, x[c]], writes=[x[c]])
            if final:
                rmsnorm_fm(p, x, 8, TG, wf, ones, x, m, ms, sd, rstd, 1024)
            p.dma("sp", xv[g % 2][0], xov[:, 0:4, gs], xt[g % 2][:, 0:4, :], reads=x[0:4])
            p.dma("act", xv[g % 2][4], xov[:, 4:8, gs], xt[g % 2][:, 4:8, :], reads=x[4:8])


def phase_inproj1(k, xin):
    p, S, W, SC = k.p, k.S, k.din, k.scr
    NG = S // 512
    with p.phase("inproj1"):
        win = p.sb("win1", [128, 8, IN1], BF16)
        wv_ = W["lin_w_in"][0].rearrange("(c p) n -> p c n", p=128)
        for c in range(8):
            p.dma("pool", win, win[:, c, :], wv_[:, c, :], writes=[win])
        wn = p.sb("wn1", [128, 8], F32)
        p.dma("sp", wn, wn[:], W["lin_norm"][0].rearrange("(c p) -> p c", p=128), writes=[wn])
        ones = p.sb("ones1", [128, 128], BF16)
        p.I("pool", lam("memset", ones[:], 1.0), writes=[ones])
        wg = []
        for d, sfx in enumerate(("f", "b")):
            w_ = p.sb("wg" + sfx, [17, 256], BF16)
            p.dma("pool", w_, w_[0:16, :], W["gla_w_gate_" + sfx][0], writes=[w_])
            p.dma("pool", w_, w_[16:17, :], W["gla_b_gate_" + sfx], writes=[w_])
            wg.append(w_)
        glT = [ring(p, "sb", "glT%d" % d, 2, [17, 512], BF16) for d in range(2)]
        for d in range(2):
            for b_ in glT[d]:
                p.I("pool", lam("memset", b_[:], 1.0), writes=[b_])
        dtb = p.sb("dtb", [128, 4, 8], F32)
        alog = p.sb("alog", [128, 4, 8], F32)
        negA = p.sb("negA", [128, 4, 8], F32)
        for j in range(4):
            p.dma("sp", dtb, dtb[:, j, 0:4], W["gdn_dt_bias_f"][0].partition_broadcast(128), writes=[dtb])
            p.dma("sp", dtb, dtb[:, j, 4:8], W["gdn_dt_bias_b"][0].partition_broadcast(128), writes=[dtb])
            p.dma("sp", alog, alog[:, j, 0:4], W["gdn_a_log_f"][0].partition_broadcast(128), writes=[alog])
            p.dma("sp", alog, alog[:, j, 4:8], W["gdn_a_log_b"][0].partition_broadcast(128), writes=[alog])
        p.I("act", lam("activation", negA[:], alog[:], AF.Exp), reads=[alog], writes=[negA])
        p.I("dve", lam("tensor_scalar", negA[:], negA[:], -1.0, None, ALU.mult), reads=[negA], writes=[negA])

        xt = ring(p, "sb", "x1t", 2, [128, 8, 512], F32)
        xv = [views(p, "x1v%d" % i, xt[i].t, 8) for i in range(2)]
        sqt = p.sb("sq1t", [128, 8, 512], BF16)
        sq = views(p, "sq1", sqt.t, 8)
        ht = p.sb("h1t", [128, 8, 512], BF16)
        h = views(p, "h1", ht.t, 8)
        sd = p.sb("sd1", [128, 512], F32)
        rstd = p.sb("rstd1", [128, 512], F32)
        stgb = ring(p, "sb", "stgb", 4, [128, 512], BF16)
        stgf = ring(p, "sb", "stgf", 3, [128, 512], F32)
        ee = ring(p, "sb", "ee", 2, [128, 512], F32)
        ll = ring(p, "sb", "ll", 2, [128, 512], F32)
        yb = p.sb("yb", [128, 4, 8], F32)
        gb = ring(p, "sb", "gb", 2, [128, 4, 16], F32)
        pp = ring(p, "ps", "p1p", 7, [128, 512])
        ms = p.ps("ms1", [128, 512])
        cnt = {"pp": 0, "stgb": 0, "stgf": 0, "e": 0}

        def nxt(nm, r):
            b = r[cnt[nm] % len(r)]
            cnt[nm] += 1
            return b

        xiv = xin.rearrange("(c p) s -> p c s", p=128)

        def load(g):
            gs = slice(g * 512, (g + 1) * 512)
            x = xv[g % 2]
            p.dma("sp", x[0], xt[g % 2][:, 0:4, :], xiv[:, 0:4, gs], writes=x[0:4])
            p.dma("act", x[4], xt[g % 2][:, 4:8, :], xiv[:, 4:8, gs], writes=x[4:8])

        def fm_tile(col0, ncols=128):
            ps = nxt("pp", pp)
            mm(p, ps, ps[0:ncols, :], [(win[:, c, col0:col0 + ncols], h[c][:]) for c in range(8)], reads=[win] + h)
            return ps

        load(0)
        for g in range(NG):
            gs = slice(g * 512, (g + 1) * 512)
            if g + 1 < NG:
                load(g + 1)
            x = xv[g % 2]
            rmsnorm_fm(p, x, 8, 512, wn, ones, h, sq, ms, sd, rstd, 1024)
            for t in range(4):
                ps = fm_tile(t * 128)
                st = nxt("stgb", stgb)
                p.I("act", lam("copy", st[:], ps[:]), reads=[ps], writes=[st])
                dst = SC["CQT"] if t < 2 else SC["CKT"]
                p.dma("sp", st, dst[(t % 2) * 128:(t % 2 + 1) * 128, gs], st[:], reads=[st])
            for t in range(8):
                ps = fm_tile(1024 + t * 128 if t < 4 else 2592 + (t - 4) * 128)
                st = nxt("stgb", stgb)
                p.I("act", lam("activation", st[:], ps[:], AF.Silu), reads=[ps], writes=[st])
                dst = SC["SG"] if t < 4 else SC["SZ"]
                p.dma("sp", st, dst[(t % 4) * 128:(t % 4 + 1) * 128, gs], st[:], reads=[st])
            for t in range(8):
                ps = fm_tile(1568 + t * 128)
                st = nxt("stgf", stgf)
                p.I("dve", lam("tensor_copy", st[:], ps[:]), reads=[ps], writes=[st])
                p.dma("sp", st, SC["DRAW"][t * 128:(t + 1) * 128, gs], st[:], reads=[st])
            for j2 in range(2):
                ps = nxt("pp", pp)
                for jj in range(2):
                    j = j2 * 2 + jj
                    mm(p, ps, ps[:, jj * 256:(jj + 1) * 256],
                       [(h[c][:, j * 128:(j + 1) * 128], win[:, c, 256:512]) for c in range(8)], reads=[win] + h)
                st = nxt("stgb", stgb)
                p.I("act", lam("copy", st[:], ps[:]), reads=[ps], writes=[st])
                p.dma("sp", st, SC["CK"][g * 512 + j2 * 256:g * 512 + (j2 + 1) * 256, :].rearrange("(j q) c -> q j c", q=128),
                      st[:].rearrange("q (j c) -> q j c", c=256), reads=[st])
            for j in range(4):
                ps = nxt("pp", pp)
                mm(p, ps, ps[:], [(h[c][:, j * 128:(j + 1) * 128], win[:, c, 512:1024]) for c in range(8)], reads=[win] + h)
                st = nxt("stgb", stgb)
                p.I("act", lam("copy", st[:], ps[:]), reads=[ps], writes=[st])
                p.dma("sp", st, SC["CV"][g * 512 + j * 128:g * 512 + (j + 1) * 128, :], st[:], reads=[st])
            for d in range(2):
                gl = glT[d][g % 2]
                ps = fm_tile(1536 + 16 * d, 16)
                p.I("act", lam("copy", gl[0:16, :], ps[0:16, :]), reads=[ps], writes=[gl])
                for j2 in range(2):
                    ps = nxt("pp", pp)
                    for jj in range(2):
                        j = j2 * 2 + jj
                        mm(p, ps, ps[:, jj * 256:(jj + 1) * 256], [(gl[:, j * 128:(j + 1) * 128], wg[d][:])], reads=[gl, wg[d]])
                    e_, l_ = ee[cnt["e"] % 2], ll[cnt["e"] % 2]
                    cnt["e"] += 1
                    p.I("act", lam("activation", e_[:], ps[:], AF.Exp, scale=-1.0), reads=[ps], writes=[e_])
                    p.I("act", lam("activation", l_[:], e_[:], AF.Ln, bias=1.0), reads=[e_], writes=[l_])
                    st = nxt("stgf", stgf)
                    p.I("pool", lam("tensor_scalar", st[:], l_[:], -1.0 / 16.0, None, ALU.mult), reads=[l_], writes=[st])
                    p.dma("sp", st, SC["LA"][d, g * 512 + j2 * 256:g * 512 + (j2 + 1) * 256, :].rearrange("(j q) c -> q j c", q=128),
                          st[:].rearrange("q (j c) -> q j c", c=256), reads=[st])
            ps = nxt("pp", pp)
            for j in range(4):
                mm(p, ps, ps[:, j * 16:(j + 1) * 16],
                   [(h[c][:, j * 128:(j + 1) * 128], win[:, c, 3104:3120]) for c in range(8)], reads=[win] + h)
            psv = ps[:, 0:64].rearrange("q (j c) -> q j c", c=16)
            gb_ = gb[g % 2]
            p.I("act", lam("activation", gb_[:, :, 0:8], psv[:, :, 0:8], AF.Sigmoid), reads=[ps], writes=[gb_])
            p.I("dve", lam("tensor_tensor", yb[:], psv[:, :, 8:16], dtb[:], ALU.add), reads=[ps, dtb], writes=[yb])
            p.I("act", lam("activation", yb[:], yb[:], AF.Exp), reads=[yb], writes=[yb])
            p.I("act", lam("activation", yb[:], yb[:], AF.Ln, bias=1.0), reads=[yb], writes=[yb])
            p.I("dve", lam("tensor_tensor", gb_[:, :, 8:16], yb[:], negA[:], ALU.mult), reads=[yb, negA], writes=[gb_])
            p.dma("sp", gb_, SC["GB"][gs, :].rearrange("(j q) c -> q j c", q=128), gb_[:], reads=[gb_])


def phase_gdnprep(k):
    p, S, W, SC = k.p, k.S, k.din, k.scr
    NG = S // 512
    DRAW = SC["DRAW"]
    with p.phase("gdnprep"):
        cm = p.sb("cm6", [128, 10, 128], F32)
        p.dma("sp", cm, cm[:], W["cmat"], writes=[cm])
        ident = p.sb("ident", [128, 128], BF16)
        ones2 = p.sb("ones2", [128, 128], BF16)
        p.I("act", lam("copy", ident[:], cm[:, 6, :]), reads=[cm], writes=[ident])
        p.I("act", lam("copy", ones2[:], cm[:, 7, :]), reads=[cm], writes=[ones2])
        cw = p.sb("cw", [128, 8, 5], F32)
        cwv = W["gdn_conv"][0].rearrange("j (c p) -> p c j", p=128)
        for c in range(8):
            p.dma("sp", cw, cw[:, c, :], cwv[:, c, :], writes=[cw])
        xr = ring(p, "sb", "xr", 2, [128, 8, 516], F32)
        xrv = [views(p, "xrv%d" % i, xr[i].t, 8) for i in range(2)]
        acct = p.sb("acct", [128, 8, 512], F32)
        acc = views(p, "acc", acct.t, 8)
        yt = p.sb("y6t", [128, 4, 512], F32)
        y = views(p, "y6", yt.t, 4)
        sqt = p.sb("sq6t", [128, 4, 512], BF16)
        sq = views(p, "sq6", sqt.t, 4)
        nbt = p.sb("nbt", [128, 8, 512], BF16)
        nb = views(p, "nb", nbt.t, 8)
        sd = ring(p, "sb", "sd6", 2, [128, 512], F32)
        rn = ring(p, "sb", "rn6", 2, [128, 512], F32)
        tk = ring(p, "sb", "tk", 2, [128, 4, 256], BF16)
        tv = ring(p, "sb", "tv", 2, [128, 2, 512], BF16)
        pss = ring(p, "ps", "p6s", 2, [128, 512])
        ptk = ring(p, "ps", "ptk", 2, [128, 4, 256], BF16)
        ptv = ring(p, "ps", "ptv", 2, [128, 2, 512], BF16)
        drv = DRAW.rearrange("(c p) s -> p c s", p=128)

        def load(g):
            b = xr[g % 2]
            lo, hi = g * 512 - 2, g * 512 + 514
            dl, dh = 0, 516
            if lo < 0:
                p.I("pool", lam("memset", b[:, :, 0:2], 0.0), writes=xrv[g % 2])
                lo, dl = 0, 2
            if hi > S:
                p.I("pool", lam("memset", b[:, :, 514:516], 0.0), writes=xrv[g % 2])
                hi, dh = S, 514
            p.dma("sp", xrv[g % 2][0], b[:, 0:4, dl:dh], drv[:, 0:4, lo:hi], writes=xrv[g % 2][0:4])
            p.dma("act", xrv[g % 2][4], b[:, 4:8, dl:dh], drv[:, 4:8, lo:hi], writes=xrv[g % 2][4:8])

        load(0)
        for g in range(NG):
            gs = slice(g * 512, (g + 1) * 512)
            if g + 1 < NG:
                load(g + 1)
            x = xrv[g % 2]
            for c in range(8):
                e = "dve"
                p.I(e, lam("tensor_scalar", acc[c][:], x[c][:, 0:512], cw[:, c, 0:1], None, ALU.mult), reads=[x[c], cw], writes=[acc[c]])
                for j in range(1, 5):
                    p.I(e, lam("scalar_tensor_tensor", acc[c][:], x[c][:, j:j + 512], cw[:, c, j:j + 1], acc[c][:], ALU.mult, ALU.add),
                        reads=[x[c], cw, acc[c]], writes=[acc[c]])
            for c in range(4, 8):
                p.I("act", lam("activation", nb[c][:], acc[c][:], AF.Silu), reads=[acc[c]], writes=[nb[c]])
            for c in range(4):
                p.I("act", lam("activation", y[c][:], acc[c][:], AF.Silu), reads=[acc[c]], writes=[y[c]])
                p.I("pool", lam("tensor_tensor", sq[c][:], y[c][:], y[c][:], ALU.mult), reads=[y[c]], writes=[sq[c]])
                ps = pss[c % 2]
                mm(p, ps, ps[:], [(ones2[:], sq[c][:])], reads=[ones2, sq[c]])
                p.I("act", lam("activation", sd[c % 2][:], ps[:], AF.Sqrt, bias=EPS), reads=[ps], writes=[sd[c % 2]])
                p.I("dve", lam("reciprocal", rn[c % 2][:], sd[c % 2][:]), reads=[sd[c % 2]], writes=[rn[c % 2]])
                p.I("dve", lam("scalar_tensor_tensor", nb[c][:], y[c][:], 0.125 if c < 2 else 1.0, rn[c % 2][:], ALU.mult, ALU.mult),
                    reads=[y[c], rn[c % 2]], writes=[nb[c]])
                dst = SC["DQT"] if c < 2 else SC["DKT"]
                p.dma("sp", nb[c], dst[(c % 2) * 128:(c % 2 + 1) * 128, gs], nb[c][:], reads=[nb[c]])
            pk = ptk[g % 2]
            for j in range(4):
                for c in range(2):
                    p.I("pe", lam("transpose", pk[:, j, c * 128:(c + 1) * 128], nb[2 + c][:, j * 128:(j + 1) * 128], ident[:]),
                        reads=[nb[2 + c], ident], writes=[pk])
            tk_ = tk[g % 2]
            p.I("act", lam("copy", tk_[:], pk[:]), reads=[pk], writes=[tk_])
            p.dma("sp", tk_, SC["DK"][gs, :].rearrange("(j q) c -> q j c", q=128), tk_[:], reads=[tk_])
            for j2 in range(2):
                pv = ptv[j2]
                for jj in range(2):
                    j = j2 * 2 + jj
                    for c in range(4):
                        p.I("pe", lam("transpose", pv[:, jj, c * 128:(c + 1) * 128], nb[4 + c][:, j * 128:(j + 1) * 128], ident[:]),
                            reads=[nb[4 + c], ident], writes=[pv])
                tv_ = tv[j2]
                p.I("dve", lam("tensor_copy", tv_[:], pv[:]), reads=[pv], writes=[tv_])
                p.dma("sp", tv_, SC["DVT"][g * 512 + j2 * 256:g * 512 + (j2 + 1) * 256, :].rearrange("(j q) c -> q j c", q=128),
                      tv_[:], reads=[tv_])


def phase_gla(k):
    p, S, W, SC = k.p, k.S, k.din, k.scr
    NT = S // 128
    with p.phase("gla"):
        cm = p.sb("cm7", [128, 10, 128], F32)
        p.dma("sp", cm, cm[:], W["cmat"], writes=[cm])
        cmr = p.sb("cm7r", [128, 10, 128], FR_GLA)
        p.I("act", lam("copy", cmr[:], cm[:]), reads=[cm], writes=[cmr])
        lar = ring(p, "sb", "lar", 2, [128, 256], FR_GLA)
        ones = p.sb("ones7", [128, 128], BF16)
        p.I("pool", lam("memset", ones[:], 1.0), writes=[ones])
        gnw = p.sb("gnw", [128, 1], F32)
        p.dma("sp", gnw, gnw[:], W["gla_norm"][0].rearrange("(p o) -> p o", o=1), writes=[gnw])
        mask4 = p.sb("mask4", [128, 4, 128], F32)
        la = ring(p, "sb", "la", 2, [128, 256], F32)
        qT = ring(p, "sb", "gqT", 2, [64, 4, 128], BF16)
        kT = ring(p, "sb", "gkT", 2, [64, 4, 128], BF16)
        kt = ring(p, "sb", "gkt", 2, [128, 256], BF16)
        vt = ring(p, "sb", "gvt", 2, [128, 512], BF16)
        ofw = ring(p, "sb", "ofw", 2, [128, 4, 128], F32)
        sg = ring(p, "sb", "gsg", 2, [128, 4, 128], BF16)
        eb = ring(p, "sb", "eb", 2, [64, 4, 128], F32)
        enb = ring(p, "sb", "enb", 2, [64, 4, 128], F32)
        ebs = ring(p, "sb", "ebs", 2, [128, 256], F32)
        qin = ring(p, "sb", "qin", 2, [64, 4, 128], BF16)
        kin = ring(p, "sb", "kin", 2, [64, 4, 128], BF16)
        kout = ring(p, "sb", "kout", 2, [128, 256], BF16)
        scm = ring(p, "sb", "scm", 2, [128, 4, 128], BF16)
        osb = ring(p, "sb", "gosb", 2, [128, 4, 128], F32)
        sqb = ring(p, "sb", "gsqb", 2, [128, 4, 128], BF16)
        sdb = ring(p, "sb", "gsdb", 2, [128, 512], F32)
        rib = ring(p, "sb", "grib", 2, [128, 512], F32)
        onb = ring(p, "sb", "gonb", 2, [128, 4, 128], F32)
        ogb = ring(p, "sb", "gogb", 2, [128, 4, 128], BF16)
        state = p.sb("gstate", [64, 4, 128], F32)
        stmp = p.sb("gstmp", [64, 4, 128], F32)
        stbf = p.sb("gstbf", [64, 4, 128], BF16)
        pb = ring(p, "ps", "gpb", 2, [128, 4, 128])
        pbs = p.ps("gpbs", [128, 512])
        psc = p.ps("gpsc", [128, 4, 128])
        po = ring(p, "ps", "gpo", 2, [128, 4, 128])
        pd = p.ps("gpd", [128, 4, 128])
        pq = p.ps("gpq", [128, 512])
        cqv = SC["CQT"].rearrange("(h e) s -> e h s", e=64)
        ckv = SC["CKT"].rearrange("(h e) s -> e h s", e=64)
        ogf = SC["OGF"].rearrange("(h p) s -> p h s", p=128)
        sgv = SC["SG"].rearrange("(h p) s -> p h s", p=128)
        omx = SC["OMIX"][0:512, :].rearrange("(h p) s -> p h s", p=128)

        for d in range(2):
            Tri, TriS, last = (cmr[:, 0, :], cmr[:, 3, :], 127) if d == 0 else (cmr[:, 1, :], cmr[:, 2, :], 0)
            for h in range(4):
                p.I("act", lam("copy", mask4[:, h, :], cm[:, 0 if d == 0 else 1, :]), reads=[cm], writes=[mask4])
            p.I("pool", lam("memset", state[:], 0.0), writes=[state])
            p.I("pool", lam("memset", stbf[:], 0.0), writes=[stbf])
            order = list(range(NT)) if d == 0 else list(range(NT - 1, -1, -1))

            def load(i):
                n = order[i]
                ts = slice(n * 128, (n + 1) * 128)
                b = i % 2
                p.dma("sp", la[b], la[b][:], SC["LA"][d, ts, :], writes=[la[b]])
                p.dma("sp", qT[b], qT[b][:], cqv[:, :, ts], writes=[qT[b]])
                p.dma("sp", kT[b], kT[b][:], ckv[:, :, ts], writes=[kT[b]])
                p.dma("act", kt[b], kt[b][:], SC["CK"][ts, :], writes=[kt[b]])
                p.dma("act", vt[b], vt[b][:], SC["CV"][ts, :], writes=[vt[b]])
                if d == 1:
                    p.dma("sp", ofw[b], ofw[b][:], ogf[:, :, ts], writes=[ofw[b]])
                    p.dma("act", sg[b], sg[b][:], sgv[:, :, ts], writes=[sg[b]])

            load(0)
            for i in range(NT):
                n = order[i]
                ts = slice(n * 128, (n + 1) * 128)
                b = i % 2
                if i + 1 < NT:
                    load(i + 1)
                pb_ = pb[b]
                p.I("pool", lam("tensor_copy", lar[b][:], la[b][:]), reads=[la[b]], writes=[lar[b]])
                for h in range(4):
                    mm(p, pb_, pb_[0:64, h, :], [(lar[b][:, h * 64:(h + 1) * 64], Tri)], reads=[lar[b], cmr])
                mm(p, pbs, pbs[:, 0:256], [(TriS, lar[b][:])], reads=[lar[b], cmr])
                p.I("act", lam("activation", eb[b][:], pb_[0:64, :, :], AF.Exp), reads=[pb_], writes=[eb[b]])
                p.I("act", lam("activation", enb[b][:], pb_[0:64, :, :], AF.Exp, scale=-1.0), reads=[pb_], writes=[enb[b]])
                p.I("act", lam("activation", ebs[b][:], pbs[:, 0:256], AF.Exp), reads=[pbs], writes=[ebs[b]])
                p.I("dve", lam("scalar_tensor_tensor", qin[b][:], qT[b][:], 0.125, eb[b][:], ALU.mult, ALU.mult),
                    reads=[qT[b], eb[b]], writes=[qin[b]])
                p.I("pool", lam("tensor_tensor", kin[b][:], kT[b][:], enb[b][:], ALU.mult), reads=[kT[b], enb[b]], writes=[kin[b]])
                p.I("pool", lam("tensor_tensor", kout[b][:], kt[b][:], ebs[b][:], ALU.mult), reads=[kt[b], ebs[b]], writes=[kout[b]])
                sc_, o_ = psc, po[b]
                for h in range(4):
                    mm(p, sc_, sc_[:, h, :], [(kin[b][:, h, :], qin[b][:, h, :])], reads=[kin[b], qin[b]])
                p.I("dve", lam("tensor_tensor", scm[b][:], sc_[:], mask4[:], ALU.mult), reads=[sc_, mask4], writes=[scm[b]])
                for h in range(4):
                    mm(p, o_, o_[:, h, :], [(vt[b][:, h * 128:(h + 1) * 128], scm[b][:, h, :]),
                                            (stbf[:, h, :], qin[b][:, h, :])],
                       reads=[vt[b], scm[b], stbf, qin[b]])
                for h in range(4):
                    mm(p, pd, pd[0:64, h, :], [(kout[b][:, h * 64:(h + 1) * 64], vt[b][:, h * 128:(h + 1) * 128])],
                       reads=[kout[b], vt[b]])
                p.I("dve", lam("tensor_tensor", stmp[:], state[:], eb[b][:, :, last:last + 1].to_broadcast([64, 4, 128]), ALU.mult),
                    reads=[state, eb[b]], writes=[stmp])
                p.I("dve", lam("tensor_tensor", state[:], stmp[:], pd[0:64, :, :], ALU.add), reads=[stmp, pd], writes=[state])
                p.I("act", lam("copy", stbf[:], state[:]), reads=[state], writes=[stbf])
                if d == 0:
                    p.I("act", lam("copy", osb[b][:], o_[:]), reads=[o_], writes=[osb[b]])
                    p.dma("sp", osb[b], ogf[:, :, ts], osb[b][:], reads=[osb[b]])
                else:
                    p.I("dve", lam("tensor_tensor", osb[b][:], o_[:], ofw[b][:], ALU.add), reads=[o_, ofw[b]], writes=[osb[b]])
                    p.I("pool", lam("tensor_tensor", sqb[b][:], osb[b][:], osb[b][:], ALU.mult), reads=[osb[b]], writes=[sqb[b]])
                    mm(p, pq, pq[:], [(ones[:], sqb[b][:].rearrange("q h t -> q (h t)"))], reads=[ones, sqb[b]])
                    p.I("act", lam("activation", sdb[b][:], pq[:], AF.Sqrt, bias=EPS, scale=1.0 / 128), reads=[pq], writes=[sdb[b]])
                    p.I("dve", lam("reciprocal", rib[b][:], sdb[b][:]), reads=[sdb[b]], writes=[rib[b]])
                    p.I("dve", lam("scalar_tensor_tensor", onb[b][:].rearrange("q h t -> q (h t)"),
                                   osb[b][:].rearrange("q h t -> q (h t)"), gnw[:, 0:1], rib[b][:], ALU.mult, ALU.mult),
                        reads=[osb[b], gnw, rib[b]], writes=[onb[b]])
                    p.I("pool", lam("tensor_tensor", ogb[b][:], onb[b][:], sg[b][:], ALU.mult), reads=[onb[b], sg[b]], writes=[ogb[b]])
                    p.dma("sp", ogb[b], omx[:, :, ts], ogb[b][:], reads=[ogb[b]])
            if d == 0:
                p.barrier()


def phase_gdn(k):
    p, S, W, SC = k.p, k.S, k.din, k.scr
    NT = S // 128
    with p.phase("gdn"):
        cm = p.sb("cm8", [128, 10, 128], F32)
        p.dma("sp", cm, cm[:], W["cmat"], writes=[cm])
        onesb = p.sb("ones8b", [128, 128], BF16)
        p.I("pool", lam("memset", onesb[:], 1.0), writes=[onesb])
        ones32 = p.sb("ones8f32", [128, 128], F32)
        p.I("pool", lam("memset", ones32[:], 1.0), writes=[ones32])
        onesf = p.sb("ones8f", [128, 128], FR_GDN)
        p.I("act", lam("copy", onesf[:], ones32[:]), reads=[ones32], writes=[onesf])
        cmr = p.sb("cm8r", [128, 10, 128], FR_GDN)
        p.I("act", lam("copy", cmr[:], cm[:]), reads=[cm], writes=[cmr])
        Idf = cm[:, 6, :]
        Idr = cmr[:, 6, :]
        gnw = p.sb("dnw", [128, 1], F32)
        p.dma("sp", gnw, gnw[:], W["gdn_norm"][0].rearrange("(p o) -> p o", o=1), writes=[gnw])
        Tri4 = p.sb("Tri4", [128, 4, 128], F32)
        str4 = p.sb("str4", [128, 4, 128], F32)
        I4 = p.sb("I4", [128, 4, 128], F32)
        for h in range(4):
            p.I("act", lam("copy", I4[:, h, :], Idf), reads=[cm], writes=[I4])
        mB = p.sb("mB16", [128, 4, 128], F32)
        mO = [p.sb("mO%d" % l, [128, 4, 128], F32) for l in range(3)]
        ones4 = p.sb("ones4", [128, 4, 128], F32)
        p.I("pool", lam("memset", ones4[:], 1.0), writes=[ones4])
        for h in range(4):
            p.I("act", lam("copy", mB[:, h, :], cm[:, 8, :]), reads=[cm], writes=[mB])
            p.I("dve", lam("tensor_tensor", mO[0][:, h, :], cm[:, 9, :], cm[:, 8, :], ALU.subtract), reads=[cm], writes=[mO[0]])
            p.I("dve", lam("tensor_tensor", mO[1][:, h, :], cm[:, 7, :], cm[:, 9, :], ALU.subtract), reads=[cm], writes=[mO[1]])
            p.I("dve", lam("tensor_tensor", mO[2][:, h, :], ones4[:, h, :], cm[:, 7, :], ALU.subtract), reads=[cm, ones4], writes=[mO[2]])

        def r2(name, shape, dt):
            return ring(p, "sb", name, 2, shape, dt)

        def r4(name, shape, dt):
            return ring(p, "sb", name, 4, shape, dt)

        gbt = r4("gbt", [128, 16], F32)
        qT = r4("dqT", [64, 4, 128], BF16)
        kT = r4("dkT", [64, 4, 128], BF16)
        kt = r4("dkt", [128, 256], BF16)
        vt = r4("dvt", [128, 512], BF16)
        ofw = r4("dofw", [128, 4, 128], F32)
        sz = r4("dsz", [128, 4, 128], BF16)
        ngb = r2("ngb", [128, 8], F32)
        est = r2("est", [128, 12], F32)
        L1 = r2("L1", [128, 4, 128], FR_GDN)
        L2 = r2("L2", [128, 4, 128], FR_GDN)
        g_r = r2("g_r", [128, 4], FR_GDN)
        GT = r2("GT", [128, 4, 128], F32)
        EBr = r2("EBr", [64, 4, 128], F32)
        NBm = r2("NBm", [128, 4, 128], F32)
        tX = r2("tX", [128, 4, 128], F32)
        Xs = r2("Xs", [128, 4, 128], FR_GDN)
        XTs = r2("XTs", [128, 4, 128], FR_GDN)
        Xd = r2("Xd", [128, 4, 128], FR_GDN)
        XdT = r2("XdT", [128, 4, 128], FR_GDN)
        XoT = [r2("XoT%d" % l, [128, 4, 128], FR_GDN) for l in range(3)]
        NTb = r2("NTb", [128, 4, 128], FR_GDN)
        Ub = r2("Ub", [128, 4, 128], FR_GDN)
        Ya = ring(p, "sb", "Ya", 4, [128, 4, 128], FR_GDN)
        YTa = ring(p, "sb", "YTa", 4, [128, 4, 128], FR_GDN)
        Pa = ring(p, "sb", "Pa", 4, [128, 4, 128], FR_GDN)
        AT = r2("AT", [128, 4, 128], BF16)
        MTb = r2("MTb", [128, 4, 128], BF16)
        ke = r2("ke", [128, 256], BF16)
        kdec = r2("kdec", [128, 256], BF16)
        nw0 = r2("nw0", [64, 4, 128], BF16)
        qdT = r2("qdT", [64, 4, 128], BF16)
        vnew = r2("vnew", [128, 4, 128], BF16)
        osb = r2("dosb", [128, 4, 128], F32)
        sqb = r2("dsqb", [128, 4, 128], BF16)
        sdb = r2("dsdb", [128, 512], F32)
        rib = r2("drib", [128, 512], F32)
        onb = r2("donb", [128, 4, 128], F32)
        ogb = r2("dogb", [128, 4, 128], BF16)
        state = p.sb("dstate", [64, 4, 128], F32)
        stmp = p.sb("dstmp", [64, 4, 128], F32)
        stbf = p.sb("dstbf", [64, 4, 128], BF16)
        G = ring(p, "ps", "dG", 6, [128, 4, 128])
        M1 = p.ps("dM1", [128, 4, 128])
        M2 = p.ps("dM2", [128, 512])
        gi = [0, 0]

        def nG(b):
            t = G[3 * b + gi[b] % 3]
            gi[b] += 1
            return t

        dqv = SC["DQT"].rearrange("(h e) s -> e h s", e=64)
        dkv = SC["DKT"].rearrange("(h e) s -> e h s", e=64)
        odf = SC["ODF"].rearrange("(h p) s -> p h s", p=128)
        szv = SC["SZ"].rearrange("(h p) s -> p h s", p=128)
        omx = SC["OMIX"][512:1024, :].rearrange("(h p) s -> p h s", p=128)

        def bc(ap, n):
            return ap.unsqueeze(2).to_broadcast([128, 4, n])

        for d in range(2):
            Tri, TriS, negm = ((cmr[:, 0, :], cmr[:, 3, :], cmr[:, 4, :]) if d == 0 else
                               (cmr[:, 1, :], cmr[:, 2, :], cmr[:, 5, :]))
            Trif, strict = (cm[:, 0, :], cm[:, 2, :]) if d == 0 else (cm[:, 1, :], cm[:, 3, :])
            for h in range(4):
                p.I("act", lam("copy", Tri4[:, h, :], Trif), reads=[cm], writes=[Tri4])
                p.I("act", lam("copy", str4[:, h, :], strict), reads=[cm], writes=[str4])
            p.I("pool", lam("memset", state[:], 0.0), writes=[state])
            p.I("pool", lam("memset", stbf[:], 0.0), writes=[stbf])
            order = list(range(NT)) if d == 0 else list(range(NT - 1, -1, -1))

            def load(i):
                n = order[i]
                ts = slice(n * 128, (n + 1) * 128)
                b = i % 4
                p.dma("sp", gbt[b], gbt[b][:], SC["GB"][ts, :], writes=[gbt[b]])
                p.dma("sp", qT[b], qT[b][:], dqv[:, :, ts], writes=[qT[b]])
                p.dma("sp", kT[b], kT[b][:], dkv[:, :, ts], writes=[kT[b]])
                p.dma("act", kt[b], kt[b][:], SC["DK"][ts, :], writes=[kt[b]])
                p.dma("act", vt[b], vt[b][:], SC["DVT"][ts, :], writes=[vt[b]])
                if d == 1:
                    p.dma("sp", ofw[b], ofw[b][:], odf[:, :, ts], writes=[ofw[b]])
                    p.dma("act", sz[b], sz[b][:], szv[:, :, ts], writes=[sz[b]])

            def prep(i):
                n = order[i]
                b = i % 2
                lb = i % 4
                beta = gbt[lb][:, 4 * d:4 * d + 4]
                g = gbt[lb][:, 8 + 4 * d:12 + 4 * d]
                p.I("dve", lam("tensor_scalar", ngb[b][:, 0:4], beta, -1.0, None, ALU.mult), reads=[gbt[lb]], writes=[ngb[b]])
                p.I("dve", lam("tensor_scalar", ngb[b][:, 4:8], g, -1.0, None, ALU.mult), reads=[gbt[lb]], writes=[ngb[b]])
                p.I("pool", lam("tensor_copy", g_r[b][:], g), reads=[gbt[lb]], writes=[g_r[b]])
                mm(p, M2, M2[:, 0:4], [(Tri, g_r[b][:])], reads=[cmr, g_r[b]])
                mm(p, M2, M2[:, 4:8], [(TriS, g_r[b][:])], reads=[cmr, g_r[b]])
                mm(p, M2, M2[:, 8:12], [(onesf[:], g_r[b][:])], reads=[onesf, g_r[b]])
                p.I("act", lam("activation", est[b][:], M2[:, 0:12], AF.Exp), reads=[M2], writes=[est[b]])
                p.I("pool", lam("tensor_copy", L1[b][:], bc(g, 128)), reads=[gbt[lb]], writes=[L1[b]])
                p.I("pool", lam("tensor_tensor", L2[b][:], Tri4[:], bc(ngb[b][:, 4:8], 128), ALU.mult), reads=[Tri4, ngb[b]], writes=[L2[b]])
                Dp = nG(b)
                for h in range(4):
                    mm(p, Dp, Dp[:, h, :], [(L1[b][:, h, :], Tri), (L2[b][:, h, :], onesf[:]), (Idr, negm)],
                       reads=[L1[b], L2[b], cmr, onesf])
                for h in range(4):
                    mm(p, M1, M1[0:64, h, :], [(L1[b][:, h, 0:64], Tri)], reads=[L1[b], cmr])
                p.I("act", lam("activation", GT[b][:], Dp[:], AF.Exp), reads=[Dp], writes=[GT[b]])
                p.I("act", lam("activation", EBr[b][:], M1[0:64, :, :], AF.Exp), reads=[M1], writes=[EBr[b]])
                yield
                KK, QK = nG(b), nG(b)
                for h in range(4):
                    mm(p, KK, KK[:, h, :], [(kT[lb][:, h, :], kT[lb][:, h, :])], reads=[kT[lb]])
                    mm(p, QK, QK[:, h, :], [(kT[lb][:, h, :], qT[lb][:, h, :])], reads=[kT[lb], qT[lb]])
                yield
                p.I("dve", lam("tensor_tensor", AT[b][:], QK[:], GT[b][:], ALU.mult), reads=[QK, GT[b]], writes=[AT[b]])
                p.I("pool", lam("tensor_tensor", NBm[b][:], str4[:], bc(ngb[b][:, 0:4], 128), ALU.mult), reads=[str4, ngb[b]], writes=[NBm[b]])
                p.I("dve", lam("tensor_tensor", tX[b][:], KK[:], GT[b][:], ALU.mult), reads=[KK, GT[b]], writes=[tX[b]])
                p.I("pool", lam("tensor_tensor", Xs[b][:], tX[b][:], NBm[b][:], ALU.mult), reads=[tX[b], NBm[b]], writes=[Xs[b]])
                yield
                XTp = nG(b)
                for h in range(4):
                    mm(p, XTp, XTp[:, h, :], [(Xs[b][:, h, :], Idr)], reads=[Xs[b], cmr])
                p.I("act", lam("copy", XTs[b][:], XTp[:]), reads=[XTp], writes=[XTs[b]])
                p.I("pool", lam("tensor_tensor", Xd[b][:], Xs[b][:].bitcast(F32), mB[:], ALU.mult), reads=[Xs[b], mB], writes=[Xd[b]])
                p.I("dve", lam("tensor_tensor", XdT[b][:], XTs[b][:].bitcast(F32), mB[:], ALU.mult), reads=[XTs[b], mB], writes=[XdT[b]])
                for l in range(3):
                    p.I("pool", lam("tensor_tensor", XoT[l][b][:], XTs[b][:].bitcast(F32), mO[l][:], ALU.mult),
                        reads=[XTs[b], mO[l]], writes=[XoT[l][b]])
                p.I("pool", lam("tensor_tensor", Pa[2 * b][:], I4[:], Xd[b][:].bitcast(F32), ALU.add), reads=[I4, Xd[b]], writes=[Pa[2 * b]])
                Y, YT, P_ = Xd[b], XdT[b], Pa[2 * b]
                for lvl in range(3):
                    Yn, YTn, Pn = Ya[2 * b + lvl % 2], YTa[2 * b + lvl % 2], Pa[2 * b + (lvl + 1) % 2]
                    yield
                    pB = nG(b)
                    for h in range(4):
                        mm(p, pB, pB[:, h, :], [(Y[:, h, :], YT[:, h, :])], reads=[Y, YT])
                    p.I("act", lam("copy", YTn[:], pB[:]), reads=[pB], writes=[YTn])
                    if lvl < 2:
                        pA = nG(b)
                        for h in range(4):
                            mm(p, pA, pA[:, h, :], [(YT[:, h, :], Y[:, h, :])], reads=[Y, YT])
                        p.I("dve", lam("tensor_copy", Yn[:], pA[:]), reads=[pA], writes=[Yn])
                    yield
                    pC = nG(b)
                    for h in range(4):
                        mm(p, pC, pC[:, h, :], [(YTn[:, h, :], P_[:, h, :])], reads=[YTn, P_])
                    p.I("dve", lam("tensor_tensor", Pn[:], pC[:], P_[:].bitcast(F32), ALU.add), reads=[pC, P_], writes=[Pn])
                    Y, YT, P_ = Yn, YTn, Pn
                N_ = P_
                for l in range(3):
                    Nn = Pa[2 * b + l % 2]
                    yield
                    pT = nG(b)
                    for h in range(4):
                        mm(p, pT, pT[:, h, :], [(N_[:, h, :], Idr)], reads=[N_, cmr])
                    p.I("act", lam("copy", NTb[b][:], pT[:]), reads=[pT], writes=[NTb[b]])
                    pU = nG(b)
                    for h in range(4):
                        mm(p, pU, pU[:, h, :], [(XoT[l][b][:, h, :], N_[:, h, :])], reads=[XoT[l][b], N_])
                    p.I("dve", lam("tensor_copy", Ub[b][:], pU[:]), reads=[pU], writes=[Ub[b]])
                    yield
                    pV = nG(b)
                    for h in range(4):
                        mm(p, pV, pV[:, h, :], [(NTb[b][:, h, :], Ub[b][:, h, :])], reads=[NTb[b], Ub[b]])
                    dst_ = MTb[b] if l == 2 else Nn
                    p.I("dve", lam("tensor_tensor", dst_[:], pV[:], N_[:].bitcast(F32), ALU.add), reads=[pV, N_], writes=[dst_])
                    N_ = Nn
                yield
                ktv = kt[lb][:].rearrange("q (h e) -> q h e", e=64)
                p.I("pool", lam("tensor_tensor", ke[b][:].rearrange("q (h e) -> q h e", e=64), ktv, bc(est[b][:, 0:4], 64), ALU.mult),
                    reads=[kt[lb], est[b]], writes=[ke[b]])
                p.I("pool", lam("tensor_tensor", kdec[b][:].rearrange("q (h e) -> q h e", e=64), ktv, bc(est[b][:, 4:8], 64), ALU.mult),
                    reads=[kt[lb], est[b]], writes=[kdec[b]])
                p.I("dve", lam("tensor_tensor", qdT[b][:], qT[lb][:], EBr[b][:], ALU.mult), reads=[qT[lb], EBr[b]], writes=[qdT[b]])
                pw = nG(b)
                for h in range(4):
                    mm(p, pw, pw[0:64, h, :], [(ke[b][:, h * 64:(h + 1) * 64], MTb[b][:, h, :])], reads=[ke[b], MTb[b]])
                p.I("act", lam("mul", nw0[b][:], pw[0:64, :, :], -1.0), reads=[pw], writes=[nw0[b]])

            def scan(i):
                n = order[i]
                ts = slice(n * 128, (n + 1) * 128)
                b = i % 2
                lb = i % 4
                beta = gbt[lb][:, 4 * d:4 * d + 4]
                un = nG(b)
                for h in range(4):
                    mm(p, un, un[:, h, :], [(MTb[b][:, h, :], vt[lb][:, h * 128:(h + 1) * 128]),
                                            (nw0[b][:, h, :], stbf[:, h, :])],
                       reads=[MTb[b], vt[lb], nw0[b], stbf])
                p.I("dve", lam("tensor_tensor", vnew[b][:], un[:], bc(beta, 128), ALU.mult), reads=[un, gbt[lb]], writes=[vnew[b]])
                o_ = nG(b)
                for h in range(4):
                    mm(p, o_, o_[:, h, :], [(stbf[:, h, :], qdT[b][:, h, :]),
                                            (vnew[b][:, h, :], AT[b][:, h, :])],
                       reads=[stbf, qdT[b], vnew[b], AT[b]])
                pds = nG(b)
                for h in range(4):
                    mm(p, pds, pds[0:64, h, :], [(kdec[b][:, h * 64:(h + 1) * 64], vnew[b][:, h, :])], reads=[kdec[b], vnew[b]])
                p.I("dve", lam("tensor_tensor", stmp[:], state[:], est[b][0:64, 8:12].unsqueeze(2).to_broadcast([64, 4, 128]), ALU.mult),
                    reads=[state, est[b]], writes=[stmp])
                p.I("dve", lam("tensor_tensor", state[:], stmp[:], pds[0:64, :, :], ALU.add), reads=[stmp, pds], writes=[state])
                p.I("act", lam("copy", stbf[:], state[:]), reads=[state], writes=[stbf])
                if d == 0:
                    p.I("act", lam("copy", osb[b][:], o_[:]), reads=[o_], writes=[osb[b]])
                    p.dma("sp", osb[b], odf[:, :, ts], osb[b][:], reads=[osb[b]])
                else:
                    p.I("dve", lam("tensor_tensor", osb[b][:], o_[:], ofw[lb][:], ALU.add), reads=[o_, ofw[lb]], writes=[osb[b]])
                    p.I("pool", lam("tensor_tensor", sqb[b][:], osb[b][:], osb[b][:], ALU.mult), reads=[osb[b]], writes=[sqb[b]])
                    pq = nG(b)
                    mm(p, pq, pq[:].rearrange("q h t -> q (h t)"), [(onesb[:], sqb[b][:].rearrange("q h t -> q (h t)"))],
                       reads=[onesb, sqb[b]])
                    p.I("act", lam("activation", sdb[b][:], pq[:].rearrange("q h t -> q (h t)"), AF.Sqrt, bias=EPS, scale=1.0 / 128),
                        reads=[pq], writes=[sdb[b]])
                    p.I("dve", lam("reciprocal", rib[b][:], sdb[b][:]), reads=[sdb[b]], writes=[rib[b]])
                    p.I("dve", lam("scalar_tensor_tensor", onb[b][:].rearrange("q h t -> q (h t)"),
                                   osb[b][:].rearrange("q h t -> q (h t)"), gnw[:, 0:1], rib[b][:], ALU.mult, ALU.mult),
                        reads=[osb[b], gnw, rib[b]], writes=[onb[b]])
                    p.I("pool", lam("tensor_tensor", ogb[b][:], onb[b][:], sz[lb][:], ALU.mult), reads=[onb[b], sz[lb]], writes=[ogb[b]])
                    p.dma("sp", ogb[b], omx[:, :, ts], ogb[b][:], reads=[ogb[b]])

            load(0)
            if NT > 1:
                load(1)
            for i0 in range(0, NT, 2):
                pair = [i for i in (i0, i0 + 1) if i < NT]
                for i in (i0 + 2, i0 + 3):
                    if i < NT:
                        load(i)
                gens = [prep(i) for i in pair]
                alive = list(gens)
                while alive:
                    for gen in list(alive):
                        try:
                            next(gen)
                        except StopIteration:
                            alive.remove(gen)
                for i in pair:
                    scan(i)
            if d == 0:
                p.barrier()


ALL_PHASES = ("inproj0", "winattn", "mla", "out0", "mlp0", "inproj1", "gdnprep", "gla", "gdn", "out1", "mlp1")
_NC_CACHE = {}


def kernel(**inputs):
    x = np.asarray(inputs["x"], dtype=np.float32)
    B, S, Dm = x.shape
    if S not in _NC_CACHE:
        _NC_CACHE[S] = build(S, phases=ALL_PHASES)
    nc = _NC_CACHE[S]
    consts = host_consts(S)
    wts = {n: np.ascontiguousarray(np.asarray(inputs[n], dtype=np.float32)) for n in INPUT_SHAPES}
    in_maps = []
    for b in range(B):
        m = dict(wts)
        m.update(consts)
        m["xT"] = np.ascontiguousarray(x[b].T)
        in_maps.append(m)
    res = run_bass_kernel_spmd(nc, in_maps, core_ids=list(range(B)))
    out = np.stack([np.asarray(r["outT"]).T for r in res.results], 0)
    return np.ascontiguousarray(out, dtype=np.float32)
```

```python
import numpy as np
from contextlib import ExitStack, contextmanager
import concourse.bass as bass
import concourse.mybir as mybir
from concourse.bass_utils import run_bass_kernel_spmd

F32 = mybir.dt.float32
BF16 = mybir.dt.bfloat16
F32R = mybir.dt.float32r
FR_GLA = F32R
FR_GDN = F32R
AF = mybir.ActivationFunctionType
ALU = mybir.AluOpType
AX = mybir.AxisListType

SAME_ENGINE_SYNC = True
SAME_ENGINE_WAR = False
N_DMA_SEMS = 72


class Buf:
    __slots__ = ("name", "t", "lastw", "readers", "sem")

    def __init__(self, name, t):
        self.name = name
        self.t = t
        self.lastw = None
        self.readers = []
        self.sem = None

    def __getitem__(self, k):
        return self.t[k]


class Prog:
    def __init__(self, nc):
        self.nc = nc
        self.es = ExitStack()
        self.es.enter_context(nc.allow_low_precision(reason="bf16 matmul operands, fp32 accumulation"))
        self.es.enter_context(nc.allow_non_contiguous_dma(reason="small one-time parameter loads"))
        self.eng = {"pe": nc.tensor, "act": nc.scalar, "dve": nc.vector,
                    "pool": nc.gpsimd, "sp": nc.sync}
        self.esem = {e: self.es.enter_context(nc.semaphore("es_" + e)) for e in self.eng}
        self.ecnt = {e: 0 for e in self.eng}
        self.dsems = [self.es.enter_context(nc.semaphore("ds%d" % i)) for i in range(N_DMA_SEMS)]
        self.dcnt = [0] * N_DMA_SEMS
        self.dfree = {True: list(range(0, 16)), False: list(range(16, N_DMA_SEMS))}
        self.waited = {e: {} for e in self.eng}
        self.phase_bufs = []
        self.pes = None
        self.n_instr = 0

    @contextmanager
    def phase(self, name):
        self.pes = ExitStack()
        self.phase_bufs = []
        self.pname = name
        try:
            yield
            self.barrier()
        finally:
            for b in self.phase_bufs:
                if b.sem is not None:
                    for sw, sidx in b.sem.items():
                        self.dfree[sw].append(sidx)
            self.pes.close()
            self.pes = None

    def sb(self, name, shape, dtype):
        t = self.pes.enter_context(self.nc.sbuf_tensor(self.pname + "_" + name, list(shape), dtype))
        b = Buf(name, t)
        self.phase_bufs.append(b)
        return b

    def ps(self, name, shape, dtype=F32):
        t = self.pes.enter_context(self.nc.psum_tensor(self.pname + "_" + name, list(shape), dtype))
        b = Buf(name, t)
        self.phase_bufs.append(b)
        return b

    def view(self, name, t):
        b = Buf(name, t)
        self.phase_bufs.append(b)
        return b

    def _deps(self, e, reads, writes):
        toks = []
        for b in reads:
            if b.lastw is not None:
                toks.append(b.lastw + (True,))
        for b in writes:
            if b.lastw is not None:
                toks.append(b.lastw + (False,))
            toks.extend(t + (False,) for t in b.readers)
        w = self.waited[e]
        eng = self.eng[e]
        for (kind, key, val, raw) in toks:
            if kind == "e":
                if key == e and (not SAME_ENGINE_SYNC or e in ("pe", "sp") or (not raw and not SAME_ENGINE_WAR)):
                    continue
                if w.get(("e", key), 0) >= val:
                    continue
                w[("e", key)] = val
                eng.wait_ge(self.esem[key], val)
            else:
                if w.get(("d", key), 0) >= val:
                    continue
                w[("d", key)] = val
                eng.wait_ge(self.dsems[key], val)

    def _mark(self, tok, reads, writes):
        for b in reads:
            b.readers.append(tok)
        for b in writes:
            b.lastw = tok
            b.readers = []

    def I(self, e, fn, reads=(), writes=(), signal=True):
        self._deps(e, reads, writes)
        ins = fn(self.eng[e])
        if signal:
            self.ecnt[e] += 1
            ins.then_inc(self.esem[e], 1)
            tok = ("e", e, self.ecnt[e])
        else:
            tok = ("e", e, self.ecnt[e] + 1)
        self._mark(tok, reads, writes)
        self.n_instr += 1
        return ins

    def dma(self, q, key, out, in_, reads=(), writes=(), **kw):
        sw = (q == "pool")
        if key.sem is None:
            key.sem = {}
        if sw not in key.sem:
            key.sem[sw] = self.dfree[sw].pop()
        s = key.sem[sw]
        self._deps(q, reads, writes)
        ins = self.eng[q].dma_start(out=out, in_=in_, **kw)
        self.dcnt[s] += 16
        ins.then_inc(self.dsems[s], 16)
        self._mark(("d", s, self.dcnt[s]), reads, writes)
        self.n_instr += 1
        return ins

    def barrier(self):
        sp = self.eng["sp"]
        w = self.waited["sp"]
        for f in self.eng:
            if f != "sp" and w.get(("e", f), 0) < self.ecnt[f]:
                w[("e", f)] = self.ecnt[f]
                sp.wait_ge(self.esem[f], self.ecnt[f])
        for s in range(N_DMA_SEMS):
            if self.dcnt[s] > 0 and w.get(("d", s), 0) < self.dcnt[s]:
                w[("d", s)] = self.dcnt[s]
                sp.wait_ge(self.dsems[s], self.dcnt[s])
        self.ecnt["sp"] += 1
        sp.sem_inc(self.esem["sp"], 1)
        v = self.ecnt["sp"]
        for f in self.eng:
            if f != "sp":
                self.waited[f][("e", "sp")] = v
                self.eng[f].wait_ge(self.esem["sp"], v)
                for g in self.eng:
                    self.waited[f][("e", g)] = self.ecnt[g]
                for s in range(N_DMA_SEMS):
                    self.waited[f][("d", s)] = self.dcnt[s]

    def finish(self):
        self.es.close()


D = 1024
DFF = 4096
EPS = 1e-6
IN0 = 1440
IN1 = 3120


def lam(f, *a, **k):
    return lambda e: getattr(e, f)(*a, **k)


class K:
    def __init__(self, nc, S):
        self.nc = nc
        self.S = S
        self.p = Prog(nc)
        self.din = {}
        self.scr = {}

    def inp(self, name, shape):
        self.din[name] = self.nc.dram_tensor(name, list(shape), F32, kind="ExternalInput").ap()
        return self.din[name]

    def scratch(self, name, shape, dt):
        self.scr[name] = self.nc.dram_tensor(name, list(shape), dt).ap()
        return self.scr[name]


def mm(p, outb, out_ap, pairs, reads):
    n = len(pairs)
    for i, (l, r) in enumerate(pairs):
        p.I("pe", lam("matmul", out_ap, l, r, start=(i == 0), stop=(i == n - 1)),
            reads=reads, writes=[outb], signal=(i == n - 1))


def ring(p, kind, name, n, shape, dtype=F32):
    f = p.sb if kind == "sb" else p.ps
    return [f("%s%d" % (name, i), shape, dtype) for i in range(n)]


def rms_a(p, x, C, sq, sq_eng="pool"):
    for c in range(C):
        p.I(sq_eng, lam("tensor_tensor", sq[c][:], x[c][:], x[c][:], ALU.mult), reads=[x[c]], writes=[sq[c]])


def rms_b(p, x, C, T, wn, ones, h, sq, ms, sd, rstd, Dn):
    mm(p, ms, ms[:, 0:T], [(ones[:], sq[c][:]) for c in range(C)], reads=[ones] + sq[:C])
    p.I("act", lam("activation", sd[:, 0:T], ms[:, 0:T], AF.Sqrt, bias=EPS, scale=1.0 / Dn), reads=[ms], writes=[sd])
    p.I("dve", lam("reciprocal", rstd[:, 0:T], sd[:, 0:T]), reads=[sd], writes=[rstd])
    for c in range(C):
        p.I("dve", lam("scalar_tensor_tensor", h[c][:], x[c][:], wn[:, c:c + 1], rstd[:, 0:T], ALU.mult, ALU.mult),
            reads=[x[c], wn, rstd], writes=[h[c]])


def rmsnorm_fm(p, x, C, T, wn, ones, h, sq, ms, sd, rstd, Dn, sq_eng="pool"):
    for c in range(C):
        p.I(sq_eng, lam("tensor_tensor", sq[c][:], x[c][:], x[c][:], ALU.mult), reads=[x[c]], writes=[sq[c]])
    mm(p, ms, ms[:, 0:T], [(ones[:], sq[c][:]) for c in range(C)], reads=[ones] + sq[:C])
    p.I("act", lam("activation", sd[:, 0:T], ms[:, 0:T], AF.Sqrt, bias=EPS, scale=1.0 / Dn), reads=[ms], writes=[sd])
    p.I("dve", lam("reciprocal", rstd[:, 0:T], sd[:, 0:T]), reads=[sd], writes=[rstd])
    for c in range(C):
        p.I("dve", lam("scalar_tensor_tensor", h[c][:], x[c][:], wn[:, c:c + 1], rstd[:, 0:T], ALU.mult, ALU.mult),
            reads=[x[c], wn, rstd], writes=[h[c]])


def views(p, name, t, C):
    return [p.view("%s_%d" % (name, c), t[:, c]) for c in range(C)]


def phase_inproj0(k):
    p, S, W, SC = k.p, k.S, k.din, k.scr
    NG = S // 512
    with p.phase("inproj0"):
        win = p.sb("win", [128, 8, IN0], BF16)
        p.dma("pool", win, win[:], W["att_w_in"][0].rearrange("(c p) n -> p c n", p=128), writes=[win])
        wuq = p.sb("wuq", [128, 3, 768], BF16)
        p.dma("pool", wuq, wuq[:], W["mla_w_uq"][0].rearrange("(c p) n -> p c n", p=128), writes=[wuq])
        wukv = p.sb("wukv", [128, 2, 1024], BF16)
        p.dma("pool", wukv, wukv[:], W["mla_w_ukv"][0].rearrange("(c p) n -> p c n", p=128), writes=[wukv])
        wukv4 = wukv.t[:].rearrange("p c (h e) -> p c h e", e=128)
        wkn = p.sb("wkn", [128, 2, 512], BF16)
        wv = p.sb("wv", [128, 2, 512], BF16)
        for c in range(2):
            p.I("act", lam("copy", wkn[:, c, :].rearrange("p (h e) -> p h e", e=64), wukv4[:, c, :, 0:64]), reads=[wukv], writes=[wkn])
            p.I("act", lam("copy", wv[:, c, :].rearrange("p (h e) -> p h e", e=64), wukv4[:, c, :, 64:128]), reads=[wukv], writes=[wv])
        wksw = p.sb("wksw", [128, 8, 32], BF16)
        p.I("act", lam("mul", wksw[:, :, 0:16], win[:, :, 1424:1440], -1.0), reads=[win], writes=[wksw])
        p.I("act", lam("copy", wksw[:, :, 16:32], win[:, :, 1408:1424]), reads=[win], writes=[wksw])
        wuqsw = p.sb("wuqsw", [128, 3, 8, 32], BF16)
        wuq4 = wuq.t[:].rearrange("p c (h e) -> p c h e", e=96)
        for c in range(3):
            p.I("act", lam("mul", wuqsw[:, c, :, 0:16], wuq4[:, c, :, 80:96], -1.0), reads=[wuq], writes=[wuqsw])
            p.I("act", lam("copy", wuqsw[:, c, :, 16:32], wuq4[:, c, :, 64:80]), reads=[wuq], writes=[wuqsw])
        wn_att = p.sb("wn_att", [128, 8], F32)
        p.dma("sp", wn_att, wn_att[:], W["att_norm"][0].rearrange("(c p) -> p c", p=128), writes=[wn_att])
        wn_q = p.sb("wn_q", [128, 3], F32)
        p.dma("sp", wn_q, wn_q[:], W["mla_q_norm"][0].rearrange("(c p) -> p c", p=128), writes=[wn_q])
        wn_kv = p.sb("wn_kv", [128, 2], F32)
        p.dma("sp", wn_kv, wn_kv[:], W["mla_kv_norm"][0].rearrange("(c p) -> p c", p=128), writes=[wn_kv])
        ones = p.sb("ones", [128, 128], BF16)
        p.I("pool", lam("memset", ones[:], 1.0), writes=[ones])

        xt = ring(p, "sb", "xt", 2, [128, 8, 512], F32)
        xv = [views(p, "xv%d" % i, xt[i].t, 8) for i in range(2)]
        sqt = p.sb("sqt", [128, 8, 512], BF16)
        sq = views(p, "sq", sqt.t, 8)
        ht = p.sb("ht", [128, 8, 512], BF16)
        h = views(p, "h", ht.t, 8)
        sd = p.sb("sd", [128, 512], F32)
        rstd = p.sb("rstd", [128, 512], F32)
        cqt = p.sb("cqt", [128, 5, 512], F32)
        cq = views(p, "cq", cqt.t, 5)
        cqnt = p.sb("cqnt", [128, 5, 512], BF16)
        cqn = views(p, "cqn", cqnt.t, 5)
        csq = ring(p, "sb", "csq", 2, [128, 2, 512], F32)
        csk = ring(p, "sb", "csk", 2, [32, 2, 512], F32)
        stg = ring(p, "sb", "stg", 4, [128, 512], BF16)
        qst = ring(p, "sb", "qst", 3, [96, 512], BF16)
        t1 = ring(p, "sb", "t1", 2, [128, 512], F32)
        t2 = ring(p, "sb", "t2", 2, [128, 512], F32)
        kst = ring(p, "sb", "kst", 2, [32, 512], BF16)
        pp = ring(p, "ps", "pp", 7, [128, 512])
        ms = p.ps("ms", [128, 512])
        cnt = {"pp": 0, "stg": 0, "qst": 0, "t": 0}

        def nxt(nm, r):
            b = r[cnt[nm] % len(r)]
            cnt[nm] += 1
            return b

        xTv = W["xT"].rearrange("(c p) s -> p c s", p=128)
        rope = W["rope_cs"]
        MQ, MK, KPE, MV, AQKT, AV = SC["MQ"], SC["MK"], SC["KPE"], SC["MV"], SC["AQKT"], SC["AV"]
        MKf = MK.rearrange("h d s -> (h d) s")
        def load(g):
            gs = slice(g * 512, (g + 1) * 512)
            x = xv[g % 2]
            p.dma("sp", x[0], xt[g % 2][:, 0:4, :], xTv[:, 0:4, gs], writes=x[0:4])
            p.dma("act", x[4], xt[g % 2][:, 4:8, :], xTv[:, 4:8, gs], writes=x[4:8])
            cq_, ck_ = csq[g % 2], csk[g % 2]
            p.dma("sp", cq_, cq_[64:96, :, :], rope[:, :, gs].rearrange("t r s -> r t s"), writes=[cq_])
            p.dma("sp", ck_, ck_[:, :, :], rope[:, :, gs].rearrange("t r s -> r t s"), writes=[ck_])

        load(0)
        for g in range(NG):
            gs = slice(g * 512, (g + 1) * 512)
            x = xv[g % 2]
            cq_, ck_ = csq[g % 2], csk[g % 2]
            if g + 1 < NG:
                load(g + 1)
            rmsnorm_fm(p, x, 8, 512, wn_att, ones, h, sq, ms, sd, rstd, 1024)
            for t in range(11):
                if t == 5:
                    continue
                ps = nxt("pp", pp)
                mm(p, ps, ps[:], [(win[:, c, t * 128:(t + 1) * 128], h[c][:]) for c in range(8)], reads=[win] + h)
                if t < 5:
                    st = nxt("stg", stg)
                    p.I("act", lam("copy", st[:], ps[:]), reads=[ps], writes=[st])
                    p.dma("sp", st, AQKT[t * 128:(t + 1) * 128, gs], st[:], reads=[st])
                else:
                    p.I("act", lam("copy", cq[t - 6][:], ps[:]), reads=[ps], writes=[cq[t - 6]])
            ps = nxt("pp", pp)
            for j in range(4):
                mm(p, ps, ps[:, j * 128:(j + 1) * 128],
                   [(h[c][:, j * 128:(j + 1) * 128], win[:, c, 640:768]) for c in range(8)], reads=[win] + h)
            st = nxt("stg", stg)
            p.I("act", lam("copy", st[:], ps[:]), reads=[ps], writes=[st])
            p.dma("sp", st, AV[gs, :].rearrange("(j q) v -> q j v", q=128),
                  st[:].rearrange("q (j v) -> q j v", v=128), reads=[st])
            psk, pskw = nxt("pp", pp), nxt("pp", pp)
            mm(p, psk, psk[0:32, :], [(win[:, c, 1408:1440], h[c][:]) for c in range(8)], reads=[win] + h)
            mm(p, pskw, pskw[0:32, :], [(wksw[:, c, :], h[c][:]) for c in range(8)], reads=[wksw] + h)
            a1, a2 = nxt("t", t1), t2[(cnt["t"] - 1) % 2]
            p.I("dve", lam("tensor_tensor", a1[0:32, :], psk[0:32, :], ck_[:, 0, :], ALU.mult), reads=[psk, ck_], writes=[a1])
            p.I("dve", lam("tensor_tensor", a2[0:32, :], pskw[0:32, :], ck_[:, 1, :], ALU.mult), reads=[pskw, ck_], writes=[a2])
            ks = kst[g % 2]
            p.I("pool", lam("tensor_tensor", ks[:], a1[0:32, :], a2[0:32, :], ALU.add), reads=[a1, a2], writes=[ks])
            p.dma("sp", ks, KPE[:, gs], ks[:], reads=[ks])
            rmsnorm_fm(p, cq[0:3], 3, 512, wn_q, ones, cqn[0:3], sq[0:3], ms, sd, rstd, 384)
            rmsnorm_fm(p, cq[3:5], 2, 512, wn_kv, ones, cqn[3:5], sq[0:2], ms, sd, rstd, 256)
            for hh in range(8):
                pa, pb = nxt("pp", pp), nxt("pp", pp)
                mm(p, pa, pa[0:96, :], [(wuq[:, c, hh * 96:(hh + 1) * 96], cqn[c][:]) for c in range(3)], reads=[wuq] + cqn[0:3])
                mm(p, pb, pb[64:96, :], [(wuqsw[:, c, hh, :], cqn[c][:]) for c in range(3)], reads=[wuqsw] + cqn[0:3])
                qs = nxt("qst", qst)
                a1, a2 = nxt("t", t1), t2[(cnt["t"] - 1) % 2]
                p.I("act", lam("copy", qs[0:64, :], pa[0:64, :]), reads=[pa], writes=[qs])
                p.I("dve", lam("tensor_tensor", a1[64:96, :], pa[64:96, :], cq_[64:96, 0, :], ALU.mult), reads=[pa, cq_], writes=[a1])
                p.I("dve", lam("tensor_tensor", a2[64:96, :], pb[64:96, :], cq_[64:96, 1, :], ALU.mult), reads=[pb, cq_], writes=[a2])
                p.I("pool", lam("tensor_tensor", qs[64:96, :], a1[64:96, :], a2[64:96, :], ALU.add), reads=[a1, a2], writes=[qs])
                p.dma("sp", qs, MQ[hh, :, gs], qs[:], reads=[qs])
            for j in range(4):
                ps = nxt("pp", pp)
                mm(p, ps, ps[:], [(wkn[:, c, j * 128:(j + 1) * 128], cqn[3 + c][:]) for c in range(2)], reads=[wkn] + cqn[3:5])
                st = nxt("stg", stg)
                p.I("act", lam("copy", st[:], ps[:]), reads=[ps], writes=[st])
                p.dma("sp", st, MKf[j * 128:(j + 1) * 128, gs], st[:], reads=[st])
            for jt in range(4):
                ps = nxt("pp", pp)
                mm(p, ps, ps[:], [(cqn[3 + c][:, jt * 128:(jt + 1) * 128], wv[:, c, :]) for c in range(2)],
                   reads=[wv] + cqn[3:5])
                st = nxt("stg", stg)
                p.I("act", lam("copy", st[:], ps[:]), reads=[ps], writes=[st])
                p.dma("sp", st, MV[g * 512 + jt * 128:g * 512 + (jt + 1) * 128, :], st[:], reads=[st])


INPUT_SHAPES = {
    "att_norm": [1, 1024], "att_w_in": [1, 1024, 1440], "att_sink": [1, 8], "mla_q_norm": [1, 384],
    "mla_w_uq": [1, 384, 768], "mla_kv_norm": [1, 256], "mla_w_ukv": [1, 256, 1024], "att_w_out": [1, 1024, 1024],
    "lin_norm": [1, 1024], "lin_w_in": [1, 1024, 3120], "gla_w_gate_f": [1, 16, 256], "gla_b_gate_f": [1, 256],
    "gla_w_gate_b": [1, 16, 256], "gla_b_gate_b": [1, 256], "gla_norm": [1, 128], "gdn_conv": [1, 5, 1024],
    "gdn_a_log_f": [1, 4], "gdn_dt_bias_f": [1, 4], "gdn_a_log_b": [1, 4], "gdn_dt_bias_b": [1, 4],
    "gdn_norm": [1, 128], "lin_w_out": [1, 1024, 1024], "mlp_norm": [2, 1024], "mlp_w1": [2, 1024, 4096],
    "mlp_w2": [2, 4096, 1024], "final_norm": [1024],
}


def build(S, phases=("inproj0",), debug_outs=()):
    nc = bass.Bass("TRN2", target_bir_lowering=False)
    k = K(nc, S)
    k.inp("xT", [1024, S])
    for n, sh in INPUT_SHAPES.items():
        k.inp(n, sh)
    k.inp("rope_cs", [2, 32, S])
    outT = nc.dram_tensor("outT", [1024, S], F32, kind="ExternalOutput").ap()
    k.outT = outT

    def scr(name, shape, dt):
        if name in debug_outs:
            k.scr[name] = nc.dram_tensor(name, list(shape), dt, kind="ExternalOutput").ap()
        else:
            k.scratch(name, shape, dt)

    scr("AQKT", [640, S], BF16)
    scr("AV", [S, 128], BF16)
    scr("MQ", [8, 96, S], BF16)
    scr("MK", [8, 64, S], BF16)
    scr("KPE", [32, S], BF16)
    scr("MV", [S, 512], BF16)
    scr("OMIX", [1024, S], BF16)
    scr("XS", [1024, S], F32)
    scr("XS1", [1024, S], F32)
    k.inp("win_mask", [128, 8, 384])
    scr("CQT", [256, S], BF16)
    scr("CKT", [256, S], BF16)
    scr("CK", [S, 256], BF16)
    scr("CV", [S, 512], BF16)
    scr("SG", [512, S], BF16)
    scr("SZ", [512, S], BF16)
    scr("LA", [2, S, 256], F32)
    scr("DRAW", [1024, S], F32)
    scr("GB", [S, 16], F32)
    scr("OGF", [512, S], F32)
    scr("ODF", [512, S], F32)
    scr("DQT", [256, S], BF16)
    scr("DKT", [256, S], BF16)
    scr("DK", [S, 256], BF16)
    scr("DVT", [S, 512], BF16)
    k.inp("cmat", [128, 10, 128])
    if "inproj0" in phases:
        phase_inproj0(k)
    if "winattn" in phases:
        phase_winattn(k)
    if "mla" in phases:
        phase_mla(k)
    if "out0" in phases:
        phase_outproj(k, "out0", k.din["att_w_out"][0], k.din["xT"], k.scr["XS1"])
    if "mlp0" in phases:
        phase_mlp(k, 0, k.scr["XS1"], k.scr["XS"], False)
    if "inproj1" in phases:
        phase_inproj1(k, k.scr["XS"] if "mlp0" in phases else k.din["xT"])
    if "gdnprep" in phases:
        phase_gdnprep(k)
    if "gla" in phases:
        phase_gla(k)
    if "gdn" in phases:
        phase_gdn(k)
    if "out1" in phases:
        phase_outproj(k, "out1", k.din["lin_w_out"][0], k.scr["XS"] if "mlp0" in phases else k.din["xT"], k.scr["XS1"])
    if "mlp1" in phases:
        phase_mlp(k, 1, k.scr["XS1"], k.outT, True)
    print("[kernel] instr=%d eng counts=%s max dma sem=%d" % (k.p.n_instr, k.p.ecnt, max(k.p.dcnt)), flush=True)
    k.p.finish()
    return nc


def host_consts(S):
    half = 16
    inv = (10000.0 ** (-np.arange(half, dtype=np.float32) / half)).astype(np.float32)
    ang = np.arange(S, dtype=np.float32)[:, None] * inv[None, :]
    cos, sin = np.cos(ang).astype(np.float32).T, np.sin(ang).astype(np.float32).T
    rope_cs = np.stack([np.concatenate([cos, cos], 0), np.concatenate([sin, sin], 0)], 0)
    s_ = np.arange(128)[:, None, None]
    j_ = np.arange(3)[None, :, None]
    t_ = np.arange(128)[None, None, :]
    dist = np.abs(t_ - s_ - (j_ - 1) * 128).astype(np.float32)
    slopes = np.array([2.0 ** (-8.0 * (h + 1) / 8) for h in range(8)], dtype=np.float32)
    wm = np.exp(-slopes[None, :, None, None] * dist[:, None]) * (dist[:, None] <= 128)
    i_ = np.arange(128)[:, None]
    jj = np.arange(128)[None, :]
    LE, GE, LT, GT = (i_ <= jj), (i_ >= jj), (i_ < jj), (i_ > jj)
    blk = (i_ // 64) == (jj // 64)
    blk16 = (i_ // 16) == (jj // 16)
    blk32 = (i_ // 32) == (jj // 32)
    cm = np.stack([LE, GE, LT, GT, np.where(LE, 0.0, -30000.0), np.where(GE, 0.0, -30000.0), i_ == jj, blk, blk16, blk32], 1)
    return {"cmat": np.ascontiguousarray(cm, dtype=np.float32),
            "rope_cs": np.ascontiguousarray(rope_cs, dtype=np.float32),
            "win_mask": np.ascontiguousarray(wm.reshape(128, 8, 384), dtype=np.float32)}


def phase_winattn(k):
    p, S, W, SC = k.p, k.S, k.din, k.scr
    NB = S // 128
    AQKT, AV, OMIX = SC["AQKT"], SC["AV"], SC["OMIX"]
    with p.phase("winattn"):
        mask = p.sb("mask", [128, 8, 384], F32)
        p.dma("sp", mask, mask[:], W["win_mask"], writes=[mask])
        sink = p.sb("sink", [64, 8], F32)
        p.dma("sp", sink, sink[:], W["att_sink"][0].partition_broadcast(64), writes=[sink])
        es = p.sb("es", [64, 8], F32)
        p.I("act", lam("activation", es[:], sink[:], AF.Exp), reads=[sink], writes=[es])
        ones64 = p.sb("ones64", [128, 64], BF16)
        p.I("pool", lam("memset", ones64[:], 1.0), writes=[ones64])
        kT = ring(p, "sb", "kT", 2, [64, S], BF16)
        Vt = ring(p, "sb", "Vt", 2, [128, NB, 64], BF16)
        qT = ring(p, "sb", "qT", 2, [64, S], BF16)
        Eb = ring(p, "sb", "Eb", 3, [128, 384], BF16)
        Pb = ring(p, "sb", "Pb", 3, [128, 384], BF16)
        den = ring(p, "sb", "den", 2, [64, 512], F32)
        rinv = ring(p, "sb", "rinv", 2, [64, 512], F32)
        ob = ring(p, "sb", "ob", 2, [64, 512], BF16)
        pss = ring(p, "ps", "pss", 3, [128, 384])
        po = ring(p, "ps", "po", 2, [64, 512])
        pr = ring(p, "ps", "pr", 2, [64, 512])
        LOOK = 2
        its = [(gk, hl, n) for gk in range(2) for hl in range(4) for n in range(NB)]

        def load_kv(gk):
            kt, vt = kT[gk % 2], Vt[gk % 2]
            p.dma("sp", kt, kt[:], AQKT[512 + gk * 64:512 + (gk + 1) * 64, :], writes=[kt])
            p.dma("sp", vt, vt[:], AV[:, gk * 64:(gk + 1) * 64].rearrange("(n q) v -> q n v", q=128), writes=[vt])

        def load_q(hh):
            qt = qT[hh % 2]
            p.dma("act", qt, qt[:], AQKT[hh * 64:(hh + 1) * 64, :], writes=[qt])

        def scores(i):
            gk, hl, n = its[i]
            hh = gk * 4 + hl
            kt, qt = kT[gk % 2], qT[hh % 2]
            s_ = pss[i % 3]
            for j in range(3):
                if 0 <= n - 1 + j < NB:
                    mm(p, s_, s_[:, j * 128:(j + 1) * 128],
                       [(kt[:, (n - 1 + j) * 128:(n + j) * 128], qt[:, n * 128:(n + 1) * 128])], reads=[kt, qt])

        pending = []

        def finalize(hh, nq, i2):
            o_, r_ = po[i2], pr[i2]
            p.I("dve", lam("tensor_scalar", den[i2][:], r_[:], es[:, hh:hh + 1], None, ALU.add), reads=[r_, es], writes=[den[i2]])
            p.I("dve", lam("reciprocal", rinv[i2][:], den[i2][:]), reads=[den[i2]], writes=[rinv[i2]])
            p.I("dve", lam("tensor_tensor", ob[i2][:], o_[:], rinv[i2][:], ALU.mult), reads=[o_, rinv[i2]], writes=[ob[i2]])
            p.dma("sp", ob[i2], OMIX[hh * 64:(hh + 1) * 64, nq * 128:(nq + 4) * 128], ob[i2][:], reads=[ob[i2]])

        load_kv(0)
        load_q(0)
        for i in range(min(LOOK, len(its))):
            scores(i)
        for i, (gk, hl, n) in enumerate(its):
            hh = gk * 4 + hl
            if n == 0:
                if hl == 0 and gk + 1 < 2:
                    load_kv(gk + 1)
                if hh + 1 < 8:
                    load_q(hh + 1)
            if i + LOOK < len(its):
                scores(i + LOOK)
            vt = Vt[gk % 2]
            nq = (n // 4) * 4
            i2 = (n // 4) % 2
            o_, r_ = po[i2], pr[i2]
            js = [j for j in range(3) if 0 <= n - 1 + j < NB]
            s_, e_, p_ = pss[i % 3], Eb[i % 3], Pb[i % 3]
            c0, c1 = js[0] * 128, (js[-1] + 1) * 128
            p.I("act", lam("activation", e_[:, c0:c1], s_[:, c0:c1], AF.Exp, scale=0.125), reads=[s_], writes=[e_])
            p.I("pool", lam("tensor_tensor", p_[:, c0:c1], e_[:, c0:c1], mask[:, hh, c0:c1], ALU.mult),
                reads=[e_, mask], writes=[p_])
            cs = slice((n - nq) * 128, (n - nq + 1) * 128)
            mm(p, o_, o_[:, cs], [(vt[:, n - 1 + j, :], p_[:, j * 128:(j + 1) * 128]) for j in js], reads=[vt, p_])
            mm(p, r_, r_[:, cs], [(ones64[:], p_[:, j * 128:(j + 1) * 128]) for j in js], reads=[ones64, p_])
            if n % 4 == 1 and pending:
                finalize(*pending.pop(0))
            if n % 4 == 3:
                pending.append((hh, nq, i2))
        while pending:
            finalize(*pending.pop(0))


def phase_mla(k):
    p, S, W, SC = k.p, k.S, k.din, k.scr
    NB, NG = S // 128, S // 512
    NP = NB // 2
    MQ, MK, KPE, MV, OMIX = SC["MQ"], SC["MK"], SC["KPE"], SC["MV"], SC["OMIX"]
    scale = 96.0 ** -0.5
    LOOK = 2
    with p.phase("mla"):
        sel = p.sb("sel", [65, 64], F32)
        p.I("pool", lam("memset", sel[:], 0.0), writes=[sel])
        p.I("pool", lam("memset", sel[64:65, :], 1.0), writes=[sel])
        qT = ring(p, "sb", "mqT", 2, [96, S], BF16)
        kT = ring(p, "sb", "mkT", 2, [96, S], BF16)
        Vt = ring(p, "sb", "mV", 2, [128, NB, 65], BF16)
        for v in Vt:
            p.I("pool", lam("memset", v[:, :, 64:65], 1.0), writes=[v])
        Eb = ring(p, "sb", "mE", 3, [128, 2, 512], BF16)
        osb = ring(p, "sb", "osb", 2, [65, 512], F32)
        rinv = ring(p, "sb", "mrinv", 2, [64, 512], F32)
        ob = ring(p, "sb", "mob", 2, [64, 512], BF16)
        pss = ring(p, "ps", "mps", 3, [128, 2, 512])
        po = p.ps("mpo", [65, 512])
        pr = p.ps("mpr", [64, 512])

        def load(hh):
            qt, kt, vt = qT[hh % 2], kT[hh % 2], Vt[hh % 2]
            p.dma("sp", qt, qt[:], MQ[hh], writes=[qt])
            p.dma("sp", kt, kt[0:64, :], MK[hh], writes=[kt])
            p.dma("act", kt, kt[64:96, :], KPE, writes=[kt])
            p.dma("sp", vt, vt[:, :, 0:64], MV[:, hh * 64:(hh + 1) * 64].rearrange("(n q) v -> q n v", q=128), writes=[vt])

        its = [(hh, g, mp) for hh in range(8) for g in range(NG) for mp in range(NP)]

        def scores(i):
            hh, g, mp = its[i]
            qt, kt = qT[hh % 2], kT[hh % 2]
            s_ = pss[i % 3]
            for j in range(2):
                m = 2 * mp + j
                mm(p, s_, s_[:, j, :], [(kt[:, m * 128:(m + 1) * 128], qt[:, g * 512:(g + 1) * 512])], reads=[kt, qt])

        pending = []

        def finalize(hh, g):
            i2 = g % 2
            mm(p, pr, pr[:], [(sel[:], osb[i2][:])], reads=[sel, osb[i2]])
            p.I("dve", lam("reciprocal", rinv[i2][:], pr[:]), reads=[pr], writes=[rinv[i2]])
            p.I("dve", lam("tensor_tensor", ob[i2][:], osb[i2][0:64, :], rinv[i2][:], ALU.mult),
                reads=[osb[i2], rinv[i2]], writes=[ob[i2]])
            p.dma("sp", ob[i2], OMIX[512 + hh * 64:512 + (hh + 1) * 64, g * 512:(g + 1) * 512], ob[i2][:], reads=[ob[i2]])

        load(0)
        for i in range(min(LOOK, len(its))):
            scores(i)
        for i, (hh, g, mp) in enumerate(its):
            if g == 0 and mp == 0 and hh + 1 < 8:
                load(hh + 1)
            if i + LOOK < len(its):
                scores(i + LOOK)
            vt = Vt[hh % 2]
            s_, e_ = pss[i % 3], Eb[i % 3]
            p.I("act", lam("activation", e_[:], s_[:], AF.Exp, scale=scale), reads=[s_], writes=[e_])
            for j in range(2):
                m = 2 * mp + j
                p.I("pe", lam("matmul", po[:], vt[:, m, :], e_[:, j, :], start=(m == 0), stop=(m == NB - 1)),
                    reads=[vt, e_], writes=[po], signal=(j == 1))
            if mp == 3 and pending:
                finalize(*pending.pop(0))
            if mp == NP - 1:
                p.I("dve", lam("tensor_copy", osb[g % 2][:], po[:]), reads=[po], writes=[osb[g % 2]])
                pending.append((hh, g))
        while pending:
            finalize(*pending.pop(0))


def phase_outproj(k, name, wo_ap, xin, xout):
    p, S, W, SC = k.p, k.S, k.din, k.scr
    NG = S // 512
    OMIX = SC["OMIX"]
    with p.phase(name):
        wo = p.sb("wo", [128, 8, 1024], BF16)
        p.dma("pool", wo, wo[:], wo_ap.rearrange("(c p) n -> p c n", p=128), writes=[wo])
        xt = ring(p, "sb", "oxt", 2, [128, 8, 512], F32)
        xv = [views(p, "oxv%d" % i, xt[i].t, 8) for i in range(2)]
        om = ring(p, "sb", "om", 2, [128, 8, 512], BF16)
        pp = ring(p, "ps", "opp", 4, [128, 512])
        xiv = xin.rearrange("(c p) s -> p c s", p=128)
        xov = xout.rearrange("(c p) s -> p c s", p=128)
        omv = OMIX.rearrange("(c p) s -> p c s", p=128)

        def load(g):
            gs = slice(g * 512, (g + 1) * 512)
            p.dma("sp", xv[g % 2][0], xt[g % 2][:], xiv[:, :, gs], writes=xv[g % 2], reads=[])
            p.dma("act", om[g % 2], om[g % 2][:], omv[:, :, gs], writes=[om[g % 2]])

        load(0)
        for g in range(NG):
            gs = slice(g * 512, (g + 1) * 512)
            if g + 1 < NG:
                load(g + 1)
            x, o = xv[g % 2], om[g % 2]
            for c in range(8):
                ps = pp[(g * 8 + c) % 4]
                mm(p, ps, ps[:], [(wo[:, cc, c * 128:(c + 1) * 128], o[:, cc, :]) for cc in range(8)], reads=[wo, o])
                p.I("dve", lam("tensor_tensor", x[c][:], ps[:], x[c][:], ALU.add), reads=[ps, x[c]], writes=[x[c]])
            p.dma("sp", xv[g % 2][0], xov[:, :, gs], xt[g % 2][:], reads=x)


def phase_mlp(k, layer, xin, xout, final):
    p, S, W, SC = k.p, k.S, k.din, k.scr
    TG = 512
    NG = S // TG
    with p.phase("mlp%d" % layer):
        w1 = p.sb("w1", [128, 8, DFF], BF16)
        w1v = W["mlp_w1"][layer].rearrange("(c p) n -> p c n", p=128)
        for c in range(8):
            p.dma("pool", w1, w1[:, c, :], w1v[:, c, :], writes=[w1])
        w2 = p.sb("w2", [128, 32, D], BF16)
        w2v = W["mlp_w2"][layer].rearrange("(c p) n -> p c n", p=128)
        for c in range(0, 32, 4):
            p.dma("pool", w2, w2[:, c:c + 4, :], w2v[:, c:c + 4, :], writes=[w2])
        wn = p.sb("mwn", [128, 8], F32)
        p.dma("sp", wn, wn[:], W["mlp_norm"][layer].rearrange("(c p) -> p c", p=128), writes=[wn])
        if final:
            wf = p.sb("mwf", [128, 8], F32)
            p.dma("sp", wf, wf[:], W["final_norm"].rearrange("(c p) -> p c", p=128), writes=[wf])
        ones = p.sb("mones", [128, 128], BF16)
        p.I("pool", lam("memset", ones[:], 1.0), writes=[ones])
        xt = ring(p, "sb", "mxt", 2, [128, 8, TG], F32)
        xv = [views(p, "mxv%d" % i, xt[i].t, 8) for i in range(2)]
        mt = p.sb("mmt", [128, 8, TG], BF16)
        m = views(p, "mm_", mt.t, 8)
        at = p.sb("mat", [128, 32, TG], BF16)
        a = views(p, "ma", at.t, 32)
        sd = p.sb("msd", [128, TG], F32)
        rstd = p.sb("mrstd", [128, TG], F32)
        rr = ring(p, "sb", "mrr", 3, [128, TG], BF16)
        pp = ring(p, "ps", "mpp", 7, [128, TG])
        ms = p.ps("mms", [128, TG])
        xiv = xin.rearrange("(c p) s -> p c s", p=128)
        xov = xout.rearrange("(c p) s -> p c s", p=128)

        def load(g):
            gs = slice(g * TG, (g + 1) * TG)
            p.dma("sp", xv[g % 2][0], xt[g % 2][:, 0:4, :], xiv[:, 0:4, gs], writes=xv[g % 2][0:4])
            p.dma("act", xv[g % 2][4], xt[g % 2][:, 4:8, :], xiv[:, 4:8, gs], writes=xv[g % 2][4:8])

        load(0)
        it = 0
        for g in range(NG):
            gs = slice(g * TG, (g + 1) * TG)
            if g + 1 < NG:
                load(g + 1)
            x = xv[g % 2]
            rmsnorm_fm(p, x, 8, TG, wn, ones, m, m, ms, sd, rstd, 1024)
            for f in range(32):
                ps = pp[it % 7]
                it += 1
                mm(p, ps, ps[:], [(w1[:, c, f * 128:(f + 1) * 128], m[c][:]) for c in range(8)], reads=[w1] + m)
                r_ = rr[it % 3]
                p.I("act", lam("activation", r_[:], ps[:], AF.Relu), reads=[ps], writes=[r_])
                p.I("dve", lam("tensor_tensor", a[f][:], ps[:], r_[:], ALU.mult), reads=[ps, r_], writes=[a[f]])
            for c in range(8):
                ps = pp[it % 7]
                it += 1
                mm(p, ps, ps[:], [(w2[:, f, c * 128:(c + 1) * 128], a[f][:]) for f in range(32)], reads=[w2] + a)
                p.I("dve", lam("tensor_tensor", x[c][:], ps[:], x[c][:], ALU.add), reads=[ps, x[c]], writes=[x[c]])
            if final:
                rmsnorm_fm(p, x, 8, TG, wf, ones, x, m, ms, sd, rstd, 1024)
            p.dma("sp", xv[g % 2][0], xov[:, 0:4, gs], xt[g % 2][:, 0:4, :], reads=x[0:4])
            p.dma("act", xv[g % 2][4], xov[:, 4:8, gs], xt[g % 2][:, 4:8, :], reads=x[4:8])


def phase_inproj1(k, xin):
    p, S, W, SC = k.p, k.S, k.din, k.scr
    NG = S // 512
    with p.phase("inproj1"):
        win = p.sb("win1", [128, 8, IN1], BF16)
        wv_ = W["lin_w_in"][0].rearrange("(c p) n -> p c n", p=128)
        for c in range(8):
            p.dma("pool", win, win[:, c, :], wv_[:, c, :], writes=[win])
        wn = p.sb("wn1", [128, 8], F32)
        p.dma("sp", wn, wn[:], W["lin_norm"][0].rearrange("(c p) -> p c", p=128), writes=[wn])
        ones = p.sb("ones1", [128, 128], BF16)
        p.I("pool", lam("memset", ones[:], 1.0), writes=[ones])
        wg = []
        for d, sfx in enumerate(("f", "b")):
            w_ = p.sb("wg" + sfx, [17, 256], BF16)
            p.dma("pool", w_, w_[0:16, :], W["gla_w_gate_" + sfx][0], writes=[w_])
            p.dma("pool", w_, w_[16:17, :], W["gla_b_gate_" + sfx], writes=[w_])
            wg.append(w_)
        glT = [ring(p, "sb", "glT%d" % d, 2, [17, 512], BF16) for d in range(2)]
        for d in range(2):
            for b_ in glT[d]:
                p.I("pool", lam("memset", b_[:], 1.0), writes=[b_])
        dtb = p.sb("dtb", [128, 4, 8], F32)
        alog = p.sb("alog", [128, 4, 8], F32)
        negA = p.sb("negA", [128, 4, 8], F32)
        for j in range(4):
            p.dma("sp", dtb, dtb[:, j, 0:4], W["gdn_dt_bias_f"][0].partition_broadcast(128), writes=[dtb])
            p.dma("sp", dtb, dtb[:, j, 4:8], W["gdn_dt_bias_b"][0].partition_broadcast(128), writes=[dtb])
            p.dma("sp", alog, alog[:, j, 0:4], W["gdn_a_log_f"][0].partition_broadcast(128), writes=[alog])
            p.dma("sp", alog, alog[:, j, 4:8], W["gdn_a_log_b"][0].partition_broadcast(128), writes=[alog])
        p.I("act", lam("activation", negA[:], alog[:], AF.Exp), reads=[alog], writes=[negA])
        p.I("dve", lam("tensor_scalar", negA[:], negA[:], -1.0, None, ALU.mult), reads=[negA], writes=[negA])

        xt = ring(p, "sb", "x1t", 2, [128, 8, 512], F32)
        xv = [views(p, "x1v%d" % i, xt[i].t, 8) for i in range(2)]
        sqt = p.sb("sq1t", [128, 8, 512], BF16)
        sq = views(p, "sq1", sqt.t, 8)
        ht = p.sb("h1t", [128, 8, 512], BF16)
        h = views(p, "h1", ht.t, 8)
        sd = p.sb("sd1", [128, 512], F32)
        rstd = p.sb("rstd1", [128, 512], F32)
        stgb = ring(p, "sb", "stgb", 4, [128, 512], BF16)
        stgf = ring(p, "sb", "stgf", 3, [128, 512], F32)
        ee = ring(p, "sb", "ee", 2, [128, 512], F32)
        ll = ring(p, "sb", "ll", 2, [128, 512], F32)
        yb = p.sb("yb", [128, 4, 8], F32)
        gb = ring(p, "sb", "gb", 2, [128, 4, 16], F32)
        pp = ring(p, "ps", "p1p", 7, [128, 512])
        ms = p.ps("ms1", [128, 512])
        cnt = {"pp": 0, "stgb": 0, "stgf": 0, "e": 0}

        def nxt(nm, r):
            b = r[cnt[nm] % len(r)]
            cnt[nm] += 1
            return b

        xiv = xin.rearrange("(c p) s -> p c s", p=128)

        def load(g):
            gs = slice(g * 512, (g + 1) * 512)
            x = xv[g % 2]
            p.dma("sp", x[0], xt[g % 2][:, 0:4, :], xiv[:, 0:4, gs], writes=x[0:4])
            p.dma("act", x[4], xt[g % 2][:, 4:8, :], xiv[:, 4:8, gs], writes=x[4:8])

        def fm_tile(col0, ncols=128):
            ps = nxt("pp", pp)
            mm(p, ps, ps[0:ncols, :], [(win[:, c, col0:col0 + ncols], h[c][:]) for c in range(8)], reads=[win] + h)
            return ps

        load(0)
        for g in range(NG):
            gs = slice(g * 512, (g + 1) * 512)
            if g + 1 < NG:
                load(g + 1)
            x = xv[g % 2]
            rmsnorm_fm(p, x, 8, 512, wn, ones, h, sq, ms, sd, rstd, 1024)
            for t in range(4):
                ps = fm_tile(t * 128)
                st = nxt("stgb", stgb)
                p.I("act", lam("copy", st[:], ps[:]), reads=[ps], writes=[st])
                dst = SC["CQT"] if t < 2 else SC["CKT"]
                p.dma("sp", st, dst[(t % 2) * 128:(t % 2 + 1) * 128, gs], st[:], reads=[st])
            for t in range(8):
                ps = fm_tile(1024 + t * 128 if t < 4 else 2592 + (t - 4) * 128)
                st = nxt("stgb", stgb)
                p.I("act", lam("activation", st[:], ps[:], AF.Silu), reads=[ps], writes=[st])
                dst = SC["SG"] if t < 4 else SC["SZ"]
                p.dma("sp", st, dst[(t % 4) * 128:(t % 4 + 1) * 128, gs], st[:], reads=[st])
            for t in range(8):
                ps = fm_tile(1568 + t * 128)
                st = nxt("stgf", stgf)
                p.I("dve", lam("tensor_copy", st[:], ps[:]), reads=[ps], writes=[st])
                p.dma("sp", st, SC["DRAW"][t * 128:(t + 1) * 128, gs], st[:], reads=[st])
            for j2 in range(2):
                ps = nxt("pp", pp)
                for jj in range(2):
                    j = j2 * 2 + jj
                    mm(p, ps, ps[:, jj * 256:(jj + 1) * 256],
                       [(h[c][:, j * 128:(j + 1) * 128], win[:, c, 256:512]) for c in range(8)], reads=[win] + h)
                st = nxt("stgb", stgb)
                p.I("act", lam("copy", st[:], ps[:]), reads=[ps], writes=[st])
                p.dma("sp", st, SC["CK"][g * 512 + j2 * 256:g * 512 + (j2 + 1) * 256, :].rearrange("(j q) c -> q j c", q=128),
                      st[:].rearrange("q (j c) -> q j c", c=256), reads=[st])
            for j in range(4):
                ps = nxt("pp", pp)
                mm(p, ps, ps[:], [(h[c][:, j * 128:(j + 1) * 128], win[:, c, 512:1024]) for c in range(8)], reads=[win] + h)
                st = nxt("stgb", stgb)
                p.I("act", lam("copy", st[:], ps[:]), reads=[ps], writes=[st])
                p.dma("sp", st, SC["CV"][g * 512 + j * 128:g * 512 + (j + 1) * 128, :], st[:], reads=[st])
            for d in range(2):
                gl = glT[d][g % 2]
                ps = fm_tile(1536 + 16 * d, 16)
                p.I("act", lam("copy", gl[0:16, :], ps[0:16, :]), reads=[ps], writes=[gl])
                for j2 in range(2):
                    ps = nxt("pp", pp)
                    for jj in range(2):
                        j = j2 * 2 + jj
                        mm(p, ps, ps[:, jj * 256:(jj + 1) * 256], [(gl[:, j * 128:(j + 1) * 128], wg[d][:])], reads=[gl, wg[d]])
                    e_, l_ = ee[cnt["e"] % 2], ll[cnt["e"] % 2]
                    cnt["e"] += 1
                    p.I("act", lam("activation", e_[:], ps[:], AF.Exp, scale=-1.0), reads=[ps], writes=[e_])
                    p.I("act", lam("activation", l_[:], e_[:], AF.Ln, bias=1.0), reads=[e_], writes=[l_])
                    st = nxt("stgf", stgf)
                    p.I("pool", lam("tensor_scalar", st[:], l_[:], -1.0 / 16.0, None, ALU.mult), reads=[l_], writes=[st])
                    p.dma("sp", st, SC["LA"][d, g * 512 + j2 * 256:g * 512 + (j2 + 1) * 256, :].rearrange("(j q) c -> q j c", q=128),
                          st[:].rearrange("q (j c) -> q j c", c=256), reads=[st])
            ps = nxt("pp", pp)
            for j in range(4):
                mm(p, ps, ps[:, j * 16:(j + 1) * 16],
                   [(h[c][:, j * 128:(j + 1) * 128], win[:, c, 3104:3120]) for c in range(8)], reads=[win] + h)
            psv = ps[:, 0:64].rearrange("q (j c) -> q j c", c=16)
            gb_ = gb[g % 2]
            p.I("act", lam("activation", gb_[:, :, 0:8], psv[:, :, 0:8], AF.Sigmoid), reads=[ps], writes=[gb_])
            p.I("dve", lam("tensor_tensor", yb[:], psv[:, :, 8:16], dtb[:], ALU.add), reads=[ps, dtb], writes=[yb])
            p.I("act", lam("activation", yb[:], yb[:], AF.Exp), reads=[yb], writes=[yb])
            p.I("act", lam("activation", yb[:], yb[:], AF.Ln, bias=1.0), reads=[yb], writes=[yb])
            p.I("dve", lam("tensor_tensor", gb_[:, :, 8:16], yb[:], negA[:], ALU.mult), reads=[yb, negA], writes=[gb_])
            p.dma("sp", gb_, SC["GB"][gs, :].rearrange("(j q) c -> q j c", q=128), gb_[:], reads=[gb_])


def phase_gdnprep(k):
    p, S, W, SC = k.p, k.S, k.din, k.scr
    NG = S // 512
    DRAW = SC["DRAW"]
    with p.phase("gdnprep"):
        cm = p.sb("cm6", [128, 10, 128], F32)
        p.dma("sp", cm, cm[:], W["cmat"], writes=[cm])
        ident = p.sb("ident", [128, 128], BF16)
        ones2 = p.sb("ones2", [128, 128], BF16)
        p.I("act", lam("copy", ident[:], cm[:, 6, :]), reads=[cm], writes=[ident])
        p.I("act", lam("copy", ones2[:], cm[:, 7, :]), reads=[cm], writes=[ones2])
        cw = p.sb("cw", [128, 8, 5], F32)
        cwv = W["gdn_conv"][0].rearrange("j (c p) -> p c j", p=128)
        for c in range(8):
            p.dma("sp", cw, cw[:, c, :], cwv[:, c, :], writes=[cw])
        xr = ring(p, "sb", "xr", 2, [128, 8, 516], F32)
        xrv = [views(p, "xrv%d" % i, xr[i].t, 8) for i in range(2)]
        acct = p.sb("acct", [128, 8, 512], F32)
        acc = views(p, "acc", acct.t, 8)
        yt = p.sb("y6t", [128, 4, 512], F32)
        y = views(p, "y6", yt.t, 4)
        sqt = p.sb("sq6t", [128, 4, 512], BF16)
        sq = views(p, "sq6", sqt.t, 4)
        nbt = p.sb("nbt", [128, 8, 512], BF16)
        nb = views(p, "nb", nbt.t, 8)
        sd = ring(p, "sb", "sd6", 2, [128, 512], F32)
        rn = ring(p, "sb", "rn6", 2, [128, 512], F32)
        tk = ring(p, "sb", "tk", 2, [128, 4, 256], BF16)
        tv = ring(p, "sb", "tv", 2, [128, 2, 512], BF16)
        pss = ring(p, "ps", "p6s", 2, [128, 512])
        ptk = ring(p, "ps", "ptk", 2, [128, 4, 256], BF16)
        ptv = ring(p, "ps", "ptv", 2, [128, 2, 512], BF16)
        drv = DRAW.rearrange("(c p) s -> p c s", p=128)

        def load(g):
            b = xr[g % 2]
            lo, hi = g * 512 - 2, g * 512 + 514
            dl, dh = 0, 516
            if lo < 0:
                p.I("pool", lam("memset", b[:, :, 0:2], 0.0), writes=xrv[g % 2])
                lo, dl = 0, 2
            if hi > S:
                p.I("pool", lam("memset", b[:, :, 514:516], 0.0), writes=xrv[g % 2])
                hi, dh = S, 514
            p.dma("sp", xrv[g % 2][0], b[:, 0:4, dl:dh], drv[:, 0:4, lo:hi], writes=xrv[g % 2][0:4])
            p.dma("act", xrv[g % 2][4], b[:, 4:8, dl:dh], drv[:, 4:8, lo:hi], writes=xrv[g % 2][4:8])

        load(0)
        for g in range(NG):
            gs = slice(g * 512, (g + 1) * 512)
            if g + 1 < NG:
                load(g + 1)
            x = xrv[g % 2]
            for c in range(8):
                e = "dve"
                p.I(e, lam("tensor_scalar", acc[c][:], x[c][:, 0:512], cw[:, c, 0:1], None, ALU.mult), reads=[x[c], cw], writes=[acc[c]])
                for j in range(1, 5):
                    p.I(e, lam("scalar_tensor_tensor", acc[c][:], x[c][:, j:j + 512], cw[:, c, j:j + 1], acc[c][:], ALU.mult, ALU.add),
                        reads=[x[c], cw, acc[c]], writes=[acc[c]])
            for c in range(4, 8):
                p.I("act", lam("activation", nb[c][:], acc[c][:], AF.Silu), reads=[acc[c]], writes=[nb[c]])
            for c in range(4):
                p.I("act", lam("activation", y[c][:], acc[c][:], AF.Silu), reads=[acc[c]], writes=[y[c]])
                p.I("pool", lam("tensor_tensor", sq[c][:], y[c][:], y[c][:], ALU.mult), reads=[y[c]], writes=[sq[c]])
                ps = pss[c % 2]
                mm(p, ps, ps[:], [(ones2[:], sq[c][:])], reads=[ones2, sq[c]])
                p.I("act", lam("activation", sd[c % 2][:], ps[:], AF.Sqrt, bias=EPS), reads=[ps], writes=[sd[c % 2]])
                p.I("dve", lam("reciprocal", rn[c % 2][:], sd[c % 2][:]), reads=[sd[c % 2]], writes=[rn[c % 2]])
                p.I("dve", lam("scalar_tensor_tensor", nb[c][:], y[c][:], 0.125 if c < 2 else 1.0, rn[c % 2][:], ALU.mult, ALU.mult),
                    reads=[y[c], rn[c % 2]], writes=[nb[c]])
                dst = SC["DQT"] if c < 2 else SC["DKT"]
                p.dma("sp", nb[c], dst[(c % 2) * 128:(c % 2 + 1) * 128, gs], nb[c][:], reads=[nb[c]])
            pk = ptk[g % 2]
            for j in range(4):
                for c in range(2):
                    p.I("pe", lam("transpose", pk[:, j, c * 128:(c + 1) * 128], nb[2 + c][:, j * 128:(j + 1) * 128], ident[:]),
                        reads=[nb[2 + c], ident], writes=[pk])
            tk_ = tk[g % 2]
            p.I("act", lam("copy", tk_[:], pk[:]), reads=[pk], writes=[tk_])
            p.dma("sp", tk_, SC["DK"][gs, :].rearrange("(j q) c -> q j c", q=128), tk_[:], reads=[tk_])
            for j2 in range(2):
                pv = ptv[j2]
                for jj in range(2):
                    j = j2 * 2 + jj
                    for c in range(4):
                        p.I("pe", lam("transpose", pv[:, jj, c * 128:(c + 1) * 128], nb[4 + c][:, j * 128:(j + 1) * 128], ident[:]),
                            reads=[nb[4 + c], ident], writes=[pv])
                tv_ = tv[j2]
                p.I("dve", lam("tensor_copy", tv_[:], pv[:]), reads=[pv], writes=[tv_])
                p.dma("sp", tv_, SC["DVT"][g * 512 + j2 * 256:g * 512 + (j2 + 1) * 256, :].rearrange("(j q) c -> q j c", q=128),
                      tv_[:], reads=[tv_])


def phase_gla(k):
    p, S, W, SC = k.p, k.S, k.din, k.scr
    NT = S // 128
    with p.phase("gla"):
        cm = p.sb("cm7", [128, 10, 128], F32)
        p.dma("sp", cm, cm[:], W["cmat"], writes=[cm])
        cmr = p.sb("cm7r", [128, 10, 128], FR_GLA)
        p.I("act", lam("copy", cmr[:], cm[:]), reads=[cm], writes=[cmr])
        lar = ring(p, "sb", "lar", 2, [128, 256], FR_GLA)
        ones = p.sb("ones7", [128, 128], BF16)
        p.I("pool", lam("memset", ones[:], 1.0), writes=[ones])
        gnw = p.sb("gnw", [128, 1], F32)
        p.dma("sp", gnw, gnw[:], W["gla_norm"][0].rearrange("(p o) -> p o", o=1), writes=[gnw])
        mask4 = p.sb("mask4", [128, 4, 128], F32)
        la = ring(p, "sb", "la", 4, [128, 256], F32)
        qT = ring(p, "sb", "gqT", 4, [64, 4, 128], BF16)
        kT = ring(p, "sb", "gkT", 4, [64, 4, 128], BF16)
        kt = ring(p, "sb", "gkt", 4, [128, 256], BF16)
        vt = ring(p, "sb", "gvt", 4, [128, 512], BF16)
        ofw = ring(p, "sb", "ofw", 4, [128, 4, 128], F32)
        sg = ring(p, "sb", "gsg", 4, [128, 4, 128], BF16)
        eb = ring(p, "sb", "eb", 2, [64, 4, 128], F32)
        enb = ring(p, "sb", "enb", 2, [64, 4, 128], F32)
        ebs = ring(p, "sb", "ebs", 2, [128, 256], F32)
        qin = ring(p, "sb", "qin", 2, [64, 4, 128], BF16)
        kin = ring(p, "sb", "kin", 2, [64, 4, 128], BF16)
        kout = ring(p, "sb", "kout", 2, [128, 256], BF16)
        scm = ring(p, "sb", "scm", 2, [128, 4, 128], BF16)
        osb = ring(p, "sb", "gosb", 2, [128, 4, 128], F32)
        sqb = ring(p, "sb", "gsqb", 2, [128, 4, 128], BF16)
        sdb = ring(p, "sb", "gsdb", 2, [128, 512], F32)
        rib = ring(p, "sb", "grib", 2, [128, 512], F32)
        onb = ring(p, "sb", "gonb", 2, [128, 4, 128], F32)
        ogb = ring(p, "sb", "gogb", 2, [128, 4, 128], BF16)
        state = p.sb("gstate", [64, 4, 128], F32)
        stmp = p.sb("gstmp", [64, 4, 128], F32)
        stbf = p.sb("gstbf", [64, 4, 128], BF16)
        pb = ring(p, "ps", "gpb", 2, [128, 4, 128])
        pbs = p.ps("gpbs", [128, 512])
        psc = p.ps("gpsc", [128, 4, 128])
        po = ring(p, "ps", "gpo", 2, [128, 4, 128])
        pd = p.ps("gpd", [128, 4, 128])
        pq = p.ps("gpq", [128, 512])
        cqv = SC["CQT"].rearrange("(h e) s -> e h s", e=64)
        ckv = SC["CKT"].rearrange("(h e) s -> e h s", e=64)
        ogf = SC["OGF"].rearrange("(h p) s -> p h s", p=128)
        sgv = SC["SG"].rearrange("(h p) s -> p h s", p=128)
        omx = SC["OMIX"][0:512, :].rearrange("(h p) s -> p h s", p=128)

        for d in range(2):
            Tri, TriS, last = (cmr[:, 0, :], cmr[:, 3, :], 127) if d == 0 else (cmr[:, 1, :], cmr[:, 2, :], 0)
            for h in range(4):
                p.I("act", lam("copy", mask4[:, h, :], cm[:, 0 if d == 0 else 1, :]), reads=[cm], writes=[mask4])
            p.I("pool", lam("memset", state[:], 0.0), writes=[state])
            p.I("pool", lam("memset", stbf[:], 0.0), writes=[stbf])
            order = list(range(NT)) if d == 0 else list(range(NT - 1, -1, -1))

            def load(i):
                n = order[i]
                ts = slice(n * 128, (n + 1) * 128)
                b = i % 4
                p.dma("sp", la[b], la[b][:], SC["LA"][d, ts, :], writes=[la[b]])
                p.dma("sp", qT[b], qT[b][:], cqv[:, :, ts], writes=[qT[b]])
                p.dma("sp", kT[b], kT[b][:], ckv[:, :, ts], writes=[kT[b]])
                p.dma("act", kt[b], kt[b][:], SC["CK"][ts, :], writes=[kt[b]])
                p.dma("act", vt[b], vt[b][:], SC["CV"][ts, :], writes=[vt[b]])
                if d == 1:
                    p.dma("sp", ofw[b], ofw[b][:], ogf[:, :, ts], writes=[ofw[b]])
                    p.dma("act", sg[b], sg[b][:], sgv[:, :, ts], writes=[sg[b]])

            def prep(i):
                b = i % 2
                lb = i % 4
                pb_ = pb[b]
                p.I("pool", lam("tensor_copy", lar[b][:], la[lb][:]), reads=[la[lb]], writes=[lar[b]])
                for h in range(4):
                    mm(p, pb_, pb_[0:64, h, :], [(lar[b][:, h * 64:(h + 1) * 64], Tri)], reads=[lar[b], cmr])
                mm(p, pbs, pbs[:, 0:256], [(TriS, lar[b][:])], reads=[lar[b], cmr])
                p.I("act", lam("activation", eb[b][:], pb_[0:64, :, :], AF.Exp), reads=[pb_], writes=[eb[b]])
                p.I("act", lam("activation", enb[b][:], pb_[0:64, :, :], AF.Exp, scale=-1.0), reads=[pb_], writes=[enb[b]])
                p.I("act", lam("activation", ebs[b][:], pbs[:, 0:256], AF.Exp), reads=[pbs], writes=[ebs[b]])
                p.I("dve", lam("scalar_tensor_tensor", qin[b][:], qT[lb][:], 0.125, eb[b][:], ALU.mult, ALU.mult),
                    reads=[qT[lb], eb[b]], writes=[qin[b]])
                p.I("pool", lam("tensor_tensor", kin[b][:], kT[lb][:], enb[b][:], ALU.mult), reads=[kT[lb], enb[b]], writes=[kin[b]])
                p.I("pool", lam("tensor_tensor", kout[b][:], kt[lb][:], ebs[b][:], ALU.mult), reads=[kt[lb], ebs[b]], writes=[kout[b]])
                yield
                sc_ = psc
                for h in range(4):
                    mm(p, sc_, sc_[:, h, :], [(kin[b][:, h, :], qin[b][:, h, :])], reads=[kin[b], qin[b]])
                p.I("dve", lam("tensor_tensor", scm[b][:], sc_[:], mask4[:], ALU.mult), reads=[sc_, mask4], writes=[scm[b]])

            def scan(i):
                n = order[i]
                ts = slice(n * 128, (n + 1) * 128)
                b = i % 2
                lb = i % 4
                o_ = po[b]
                for h in range(4):
                    mm(p, o_, o_[:, h, :], [(vt[lb][:, h * 128:(h + 1) * 128], scm[b][:, h, :]),
                                            (stbf[:, h, :], qin[b][:, h, :])],
                       reads=[vt[lb], scm[b], stbf, qin[b]])
                for h in range(4):
                    mm(p, pd, pd[0:64, h, :], [(kout[b][:, h * 64:(h + 1) * 64], vt[lb][:, h * 128:(h + 1) * 128])],
                       reads=[kout[b], vt[lb]])
                p.I("dve", lam("tensor_tensor", stmp[:], state[:], eb[b][:, :, last:last + 1].to_broadcast([64, 4, 128]), ALU.mult),
                    reads=[state, eb[b]], writes=[stmp])
                p.I("dve", lam("tensor_tensor", state[:], stmp[:], pd[0:64, :, :], ALU.add), reads=[stmp, pd], writes=[state])
                p.I("act", lam("copy", stbf[:], state[:]), reads=[state], writes=[stbf])
                if d == 0:
                    p.I("act", lam("copy", osb[b][:], o_[:]), reads=[o_], writes=[osb[b]])
                    p.dma("sp", osb[b], ogf[:, :, ts], osb[b][:], reads=[osb[b]])
                else:
                    p.I("dve", lam("tensor_tensor", osb[b][:], o_[:], ofw[lb][:], ALU.add), reads=[o_, ofw[lb]], writes=[osb[b]])
                    p.I("pool", lam("tensor_tensor", sqb[b][:], osb[b][:], osb[b][:], ALU.mult), reads=[osb[b]], writes=[sqb[b]])
                    mm(p, pq, pq[:], [(ones[:], sqb[b][:].rearrange("q h t -> q (h t)"))], reads=[ones, sqb[b]])
                    p.I("act", lam("activation", sdb[b][:], pq[:], AF.Sqrt, bias=EPS, scale=1.0 / 128), reads=[pq], writes=[sdb[b]])
                    p.I("dve", lam("reciprocal", rib[b][:], sdb[b][:]), reads=[sdb[b]], writes=[rib[b]])
                    p.I("dve", lam("scalar_tensor_tensor", onb[b][:].rearrange("q h t -> q (h t)"),
                                   osb[b][:].rearrange("q h t -> q (h t)"), gnw[:, 0:1], rib[b][:], ALU.mult, ALU.mult),
                        reads=[osb[b], gnw, rib[b]], writes=[onb[b]])
                    p.I("pool", lam("tensor_tensor", ogb[b][:], onb[b][:], sg[lb][:], ALU.mult), reads=[onb[b], sg[lb]], writes=[ogb[b]])
                    p.dma("sp", ogb[b], omx[:, :, ts], ogb[b][:], reads=[ogb[b]])

            load(0)
            if NT > 1:
                load(1)
            for i0 in range(0, NT, 2):
                pair = [i for i in (i0, i0 + 1) if i < NT]
                for i in (i0 + 2, i0 + 3):
                    if i < NT:
                        load(i)
                alive = [prep(i) for i in pair]
                while alive:
                    for gen in list(alive):
                        try:
                            next(gen)
                        except StopIteration:
                            alive.remove(gen)
                for i in pair:
                    scan(i)
            if d == 0:
                p.barrier()


def phase_gdn(k):
    p, S, W, SC = k.p, k.S, k.din, k.scr
    NT = S // 128
    with p.phase("gdn"):
        cm = p.sb("cm8", [128, 10, 128], F32)
        p.dma("sp", cm, cm[:], W["cmat"], writes=[cm])
        onesb = p.sb("ones8b", [128, 128], BF16)
        p.I("pool", lam("memset", onesb[:], 1.0), writes=[onesb])
        ones32 = p.sb("ones8f32", [128, 128], F32)
        p.I("pool", lam("memset", ones32[:], 1.0), writes=[ones32])
        onesf = p.sb("ones8f", [128, 128], FR_GDN)
        p.I("act", lam("copy", onesf[:], ones32[:]), reads=[ones32], writes=[onesf])
        cmr = p.sb("cm8r", [128, 10, 128], FR_GDN)
        p.I("act", lam("copy", cmr[:], cm[:]), reads=[cm], writes=[cmr])
        Idf = cm[:, 6, :]
        Idr = cmr[:, 6, :]
        gnw = p.sb("dnw", [128, 1], F32)
        p.dma("sp", gnw, gnw[:], W["gdn_norm"][0].rearrange("(p o) -> p o", o=1), writes=[gnw])
        Tri4 = p.sb("Tri4", [128, 4, 128], F32)
        str4 = p.sb("str4", [128, 4, 128], F32)
        I4 = p.sb("I4", [128, 4, 128], F32)
        for h in range(4):
            p.I("act", lam("copy", I4[:, h, :], Idf), reads=[cm], writes=[I4])
        mB = p.sb("mB16", [128, 4, 128], F32)
        mO = [p.sb("mO%d" % l, [128, 4, 128], F32) for l in range(3)]
        ones4 = p.sb("ones4", [128, 4, 128], F32)
        p.I("pool", lam("memset", ones4[:], 1.0), writes=[ones4])
        for h in range(4):
            p.I("act", lam("copy", mB[:, h, :], cm[:, 8, :]), reads=[cm], writes=[mB])
            p.I("dve", lam("tensor_tensor", mO[0][:, h, :], cm[:, 9, :], cm[:, 8, :], ALU.subtract), reads=[cm], writes=[mO[0]])
            p.I("dve", lam("tensor_tensor", mO[1][:, h, :], cm[:, 7, :], cm[:, 9, :], ALU.subtract), reads=[cm], writes=[mO[1]])
            p.I("dve", lam("tensor_tensor", mO[2][:, h, :], ones4[:, h, :], cm[:, 7, :], ALU.subtract), reads=[cm, ones4], writes=[mO[2]])

        def r2(name, shape, dt):
            return ring(p, "sb", name, 2, shape, dt)

        def r4(name, shape, dt):
            return ring(p, "sb", name, 4, shape, dt)

        gbt = r4("gbt", [128, 16], F32)
        qT = r4("dqT", [64, 4, 128], BF16)
        kT = r4("dkT", [64, 4, 128], BF16)
        kt = r4("dkt", [128, 256], BF16)
        vt = r4("dvt", [128, 512], BF16)
        ofw = r4("dofw", [128, 4, 128], F32)
        sz = r4("dsz", [128, 4, 128], BF16)
        ngb = r2("ngb", [128, 8], F32)
        est = r2("est", [128, 12], F32)
        L1 = r2("L1", [128, 4, 128], FR_GDN)
        L2 = r2("L2", [128, 4, 128], FR_GDN)
        g_r = r2("g_r", [128, 4], FR_GDN)
        GT = r2("GT", [128, 4, 128], F32)
        EBr = r2("EBr", [64, 4, 128], F32)
        NBm = r2("NBm", [128, 4, 128], F32)
        tX = r2("tX", [128, 4, 128], F32)
        Xs = r2("Xs", [128, 4, 128], FR_GDN)
        XTs = r2("XTs", [128, 4, 128], FR_GDN)
        Xd = r2("Xd", [128, 4, 128], FR_GDN)
        XdT = r2("XdT", [128, 4, 128], FR_GDN)
        XoT = [r2("XoT%d" % l, [128, 4, 128], FR_GDN) for l in range(3)]
        NTb = r2("NTb", [128, 4, 128], FR_GDN)
        Ub = r2("Ub", [128, 4, 128], FR_GDN)
        Ya = ring(p, "sb", "Ya", 4, [128, 4, 128], FR_GDN)
        YTa = ring(p, "sb", "YTa", 4, [128, 4, 128], FR_GDN)
        Pa = ring(p, "sb", "Pa", 4, [128, 4, 128], FR_GDN)
        AT = r2("AT", [128, 4, 128], BF16)
        MTb = r2("MTb", [128, 4, 128], BF16)
        ke = r2("ke", [128, 256], BF16)
        kdec = r2("kdec", [128, 256], BF16)
        nw0 = r2("nw0", [64, 4, 128], BF16)
        qdT = r2("qdT", [64, 4, 128], BF16)
        vnew = r2("vnew", [128, 4, 128], BF16)
        osb = r2("dosb", [128, 4, 128], F32)
        sqb = r2("dsqb", [128, 4, 128], BF16)
        sdb = r2("dsdb", [128, 512], F32)
        rib = r2("drib", [128, 512], F32)
        onb = r2("donb", [128, 4, 128], F32)
        ogb = r2("dogb", [128, 4, 128], BF16)
        state = p.sb("dstate", [64, 4, 128], F32)
        stmp = p.sb("dstmp", [64, 4, 128], F32)
        stbf = p.sb("dstbf", [64, 4, 128], BF16)
        G = ring(p, "ps", "dG", 6, [128, 4, 128])
        M1 = p.ps("dM1", [128, 4, 128])
        M2 = p.ps("dM2", [128, 512])
        gi = [0, 0]

        def nG(b):
            t = G[3 * b + gi[b] % 3]
            gi[b] += 1
            return t

        dqv = SC["DQT"].rearrange("(h e) s -> e h s", e=64)
        dkv = SC["DKT"].rearrange("(h e) s -> e h s", e=64)
        odf = SC["ODF"].rearrange("(h p) s -> p h s", p=128)
        szv = SC["SZ"].rearrange("(h p) s -> p h s", p=128)
        omx = SC["OMIX"][512:1024, :].rearrange("(h p) s -> p h s", p=128)

        def bc(ap, n):
            return ap.unsqueeze(2).to_broadcast([128, 4, n])

        for d in range(2):
            Tri, TriS, negm = ((cmr[:, 0, :], cmr[:, 3, :], cmr[:, 4, :]) if d == 0 else
                               (cmr[:, 1, :], cmr[:, 2, :], cmr[:, 5, :]))
            Trif, strict = (cm[:, 0, :], cm[:, 2, :]) if d == 0 else (cm[:, 1, :], cm[:, 3, :])
            for h in range(4):
                p.I("act", lam("copy", Tri4[:, h, :], Trif), reads=[cm], writes=[Tri4])
                p.I("act", lam("copy", str4[:, h, :], strict), reads=[cm], writes=[str4])
            p.I("pool", lam("memset", state[:], 0.0), writes=[state])
            p.I("pool", lam("memset", stbf[:], 0.0), writes=[stbf])
            order = list(range(NT)) if d == 0 else list(range(NT - 1, -1, -1))

            def load(i):
                n = order[i]
                ts = slice(n * 128, (n + 1) * 128)
                b = i % 4
                p.dma("sp", gbt[b], gbt[b][:], SC["GB"][ts, :], writes=[gbt[b]])
                p.dma("sp", qT[b], qT[b][:], dqv[:, :, ts], writes=[qT[b]])
                p.dma("sp", kT[b], kT[b][:], dkv[:, :, ts], writes=[kT[b]])
                p.dma("act", kt[b], kt[b][:], SC["DK"][ts, :], writes=[kt[b]])
                p.dma("act", vt[b], vt[b][:], SC["DVT"][ts, :], writes=[vt[b]])
                if d == 1:
                    p.dma("sp", ofw[b], ofw[b][:], odf[:, :, ts], writes=[ofw[b]])
                    p.dma("act", sz[b], sz[b][:], szv[:, :, ts], writes=[sz[b]])

            def prep(i):
                n = order[i]
                b = i % 2
                lb = i % 4
                beta = gbt[lb][:, 4 * d:4 * d + 4]
                g = gbt[lb][:, 8 + 4 * d:12 + 4 * d]
                p.I("dve", lam("tensor_scalar", ngb[b][:, 0:4], beta, -1.0, None, ALU.mult), reads=[gbt[lb]], writes=[ngb[b]])
                p.I("dve", lam("tensor_scalar", ngb[b][:, 4:8], g, -1.0, None, ALU.mult), reads=[gbt[lb]], writes=[ngb[b]])
                p.I("pool", lam("tensor_copy", g_r[b][:], g), reads=[gbt[lb]], writes=[g_r[b]])
                mm(p, M2, M2[:, 0:4], [(Tri, g_r[b][:])], reads=[cmr, g_r[b]])
                mm(p, M2, M2[:, 4:8], [(TriS, g_r[b][:])], reads=[cmr, g_r[b]])
                mm(p, M2, M2[:, 8:12], [(onesf[:], g_r[b][:])], reads=[onesf, g_r[b]])
                p.I("act", lam("activation", est[b][:], M2[:, 0:12], AF.Exp), reads=[M2], writes=[est[b]])
                p.I("pool", lam("tensor_copy", L1[b][:], bc(g, 128)), reads=[gbt[lb]], writes=[L1[b]])
                p.I("pool", lam("tensor_tensor", L2[b][:], Tri4[:], bc(ngb[b][:, 4:8], 128), ALU.mult), reads=[Tri4, ngb[b]], writes=[L2[b]])
                Dp = nG(b)
                for h in range(4):
                    mm(p, Dp, Dp[:, h, :], [(L1[b][:, h, :], Tri), (L2[b][:, h, :], onesf[:]), (Idr, negm)],
                       reads=[L1[b], L2[b], cmr, onesf])
                for h in range(4):
                    mm(p, M1, M1[0:64, h, :], [(L1[b][:, h, 0:64], Tri)], reads=[L1[b], cmr])
                p.I("act", lam("activation", GT[b][:], Dp[:], AF.Exp), reads=[Dp], writes=[GT[b]])
                p.I("act", lam("activation", EBr[b][:], M1[0:64, :, :], AF.Exp), reads=[M1], writes=[EBr[b]])
                yield
                KK, QK = nG(b), nG(b)
                for h in range(4):
                    mm(p, KK, KK[:, h, :], [(kT[lb][:, h, :], kT[lb][:, h, :])], reads=[kT[lb]])
                    mm(p, QK, QK[:, h, :], [(kT[lb][:, h, :], qT[lb][:, h, :])], reads=[kT[lb], qT[lb]])
                yield
                p.I("dve", lam("tensor_tensor", AT[b][:], QK[:], GT[b][:], ALU.mult), reads=[QK, GT[b]], writes=[AT[b]])
                p.I("pool", lam("tensor_tensor", NBm[b][:], str4[:], bc(ngb[b][:, 0:4], 128), ALU.mult), reads=[str4, ngb[b]], writes=[NBm[b]])
                p.I("dve", lam("tensor_tensor", tX[b][:], KK[:], GT[b][:], ALU.mult), reads=[KK, GT[b]], writes=[tX[b]])
                p.I("pool", lam("tensor_tensor", Xs[b][:], tX[b][:], NBm[b][:], ALU.mult), reads=[tX[b], NBm[b]], writes=[Xs[b]])
                yield
                XTp = nG(b)
                for h in range(4):
                    mm(p, XTp, XTp[:, h, :], [(Xs[b][:, h, :], Idr)], reads=[Xs[b], cmr])
                p.I("act", lam("copy", XTs[b][:], XTp[:]), reads=[XTp], writes=[XTs[b]])
                p.I("pool", lam("tensor_tensor", Xd[b][:], Xs[b][:].bitcast(F32), mB[:], ALU.mult), reads=[Xs[b], mB], writes=[Xd[b]])
                p.I("dve", lam("tensor_tensor", XdT[b][:], XTs[b][:].bitcast(F32), mB[:], ALU.mult), reads=[XTs[b], mB], writes=[XdT[b]])
                for l in range(3):
                    p.I("pool", lam("tensor_tensor", XoT[l][b][:], XTs[b][:].bitcast(F32), mO[l][:], ALU.mult),
                        reads=[XTs[b], mO[l]], writes=[XoT[l][b]])
                p.I("pool", lam("tensor_tensor", Pa[2 * b][:], I4[:], Xd[b][:].bitcast(F32), ALU.add), reads=[I4, Xd[b]], writes=[Pa[2 * b]])
                Y, YT, P_ = Xd[b], XdT[b], Pa[2 * b]
                for lvl in range(3):
                    Yn, YTn, Pn = Ya[2 * b + lvl % 2], YTa[2 * b + lvl % 2], Pa[2 * b + (lvl + 1) % 2]
                    yield
                    pB = nG(b)
                    for h in range(4):
                        mm(p, pB, pB[:, h, :], [(Y[:, h, :], YT[:, h, :])], reads=[Y, YT])
                    p.I("act", lam("copy", YTn[:], pB[:]), reads=[pB], writes=[YTn])
                    if lvl < 2:
                        pA = nG(b)
                        for h in range(4):
                            mm(p, pA, pA[:, h, :], [(YT[:, h, :], Y[:, h, :])], reads=[Y, YT])
                        p.I("dve", lam("tensor_copy", Yn[:], pA[:]), reads=[pA], writes=[Yn])
                    yield
                    pC = nG(b)
                    for h in range(4):
                        mm(p, pC, pC[:, h, :], [(YTn[:, h, :], P_[:, h, :])], reads=[YTn, P_])
                    p.I("dve", lam("tensor_tensor", Pn[:], pC[:], P_[:].bitcast(F32), ALU.add), reads=[pC, P_], writes=[Pn])
                    Y, YT, P_ = Yn, YTn, Pn
                N_ = P_
                for l in range(3):
                    Nn = Pa[2 * b + l % 2]
                    yield
                    pT = nG(b)
                    for h in range(4):
                        mm(p, pT, pT[:, h, :], [(N_[:, h, :], Idr)], reads=[N_, cmr])
                    p.I("act", lam("copy", NTb[b][:], pT[:]), reads=[pT], writes=[NTb[b]])
                    pU = nG(b)
                    for h in range(4):
                        mm(p, pU, pU[:, h, :], [(XoT[l][b][:, h, :], N_[:, h, :])], reads=[XoT[l][b], N_])
                    p.I("dve", lam("tensor_copy", Ub[b][:], pU[:]), reads=[pU], writes=[Ub[b]])
                    yield
                    pV = nG(b)
                    for h in range(4):
                        mm(p, pV, pV[:, h, :], [(NTb[b][:, h, :], Ub[b][:, h, :])], reads=[NTb[b], Ub[b]])
                    dst_ = MTb[b] if l == 2 else Nn
                    p.I("dve", lam("tensor_tensor", dst_[:], pV[:], N_[:].bitcast(F32), ALU.add), reads=[pV, N_], writes=[dst_])
                    N_ = Nn
                yield
                ktv = kt[lb][:].rearrange("q (h e) -> q h e", e=64)
                p.I("pool", lam("tensor_tensor", ke[b][:].rearrange("q (h e) -> q h e", e=64), ktv, bc(est[b][:, 0:4], 64), ALU.mult),
                    reads=[kt[lb], est[b]], writes=[ke[b]])
                p.I("pool", lam("tensor_tensor", kdec[b][:].rearrange("q (h e) -> q h e", e=64), ktv, bc(est[b][:, 4:8], 64), ALU.mult),
                    reads=[kt[lb], est[b]], writes=[kdec[b]])
                p.I("dve", lam("tensor_tensor", qdT[b][:], qT[lb][:], EBr[b][:], ALU.mult), reads=[qT[lb], EBr[b]], writes=[qdT[b]])
                pw = nG(b)
                for h in range(4):
                    mm(p, pw, pw[0:64, h, :], [(ke[b][:, h * 64:(h + 1) * 64], MTb[b][:, h, :])], reads=[ke[b], MTb[b]])
                p.I("act", lam("mul", nw0[b][:], pw[0:64, :, :], -1.0), reads=[pw], writes=[nw0[b]])

            def scan(i):
                n = order[i]
                ts = slice(n * 128, (n + 1) * 128)
                b = i % 2
                lb = i % 4
                beta = gbt[lb][:, 4 * d:4 * d + 4]
                un = nG(b)
                for h in range(4):
                    mm(p, un, un[:, h, :], [(MTb[b][:, h, :], vt[lb][:, h * 128:(h + 1) * 128]),
                                            (nw0[b][:, h, :], stbf[:, h, :])],
                       reads=[MTb[b], vt[lb], nw0[b], stbf])
                p.I("dve", lam("tensor_tensor", vnew[b][:], un[:], bc(beta, 128), ALU.mult), reads=[un, gbt[lb]], writes=[vnew[b]])
                o_ = nG(b)
                for h in range(4):
                    mm(p, o_, o_[:, h, :], [(stbf[:, h, :], qdT[b][:, h, :]),
                                            (vnew[b][:, h, :], AT[b][:, h, :])],
                       reads=[stbf, qdT[b], vnew[b], AT[b]])
                pds = nG(b)
                for h in range(4):
                    mm(p, pds, pds[0:64, h, :], [(kdec[b][:, h * 64:(h + 1) * 64], vnew[b][:, h, :])], reads=[kdec[b], vnew[b]])
                p.I("dve", lam("tensor_tensor", stmp[:], state[:], est[b][0:64, 8:12].unsqueeze(2).to_broadcast([64, 4, 128]), ALU.mult),
                    reads=[state, est[b]], writes=[stmp])
                p.I("dve", lam("tensor_tensor", state[:], stmp[:], pds[0:64, :, :], ALU.add), reads=[stmp, pds], writes=[state])
                p.I("act", lam("copy", stbf[:], state[:]), reads=[state], writes=[stbf])
                if d == 0:
                    p.I("act", lam("copy", osb[b][:], o_[:]), reads=[o_], writes=[osb[b]])
                    p.dma("sp", osb[b], odf[:, :, ts], osb[b][:], reads=[osb[b]])
                else:
                    p.I("dve", lam("tensor_tensor", osb[b][:], o_[:], ofw[lb][:], ALU.add), reads=[o_, ofw[lb]], writes=[osb[b]])
                    p.I("pool", lam("tensor_tensor", sqb[b][:], osb[b][:], osb[b][:], ALU.mult), reads=[osb[b]], writes=[sqb[b]])
                    pq = nG(b)
                    mm(p, pq, pq[:].rearrange("q h t -> q (h t)"), [(onesb[:], sqb[b][:].rearrange("q h t -> q (h t)"))],
                       reads=[onesb, sqb[b]])
                    p.I("act", lam("activation", sdb[b][:], pq[:].rearrange("q h t -> q (h t)"), AF.Sqrt, bias=EPS, scale=1.0 / 128),
                        reads=[pq], writes=[sdb[b]])
                    p.I("dve", lam("reciprocal", rib[b][:], sdb[b][:]), reads=[sdb[b]], writes=[rib[b]])
                    p.I("dve", lam("scalar_tensor_tensor", onb[b][:].rearrange("q h t -> q (h t)"),
                                   osb[b][:].rearrange("q h t -> q (h t)"), gnw[:, 0:1], rib[b][:], ALU.mult, ALU.mult),
                        reads=[osb[b], gnw, rib[b]], writes=[onb[b]])
                    p.I("pool", lam("tensor_tensor", ogb[b][:], onb[b][:], sz[lb][:], ALU.mult), reads=[onb[b], sz[lb]], writes=[ogb[b]])
                    p.dma("sp", ogb[b], omx[:, :, ts], ogb[b][:], reads=[ogb[b]])

            load(0)
            if NT > 1:
                load(1)
            for i0 in range(0, NT, 2):
                pair = [i for i in (i0, i0 + 1) if i < NT]
                for i in (i0 + 2, i0 + 3):
                    if i < NT:
                        load(i)
                gens = [prep(i) for i in pair]
                alive = list(gens)
                while alive:
                    for gen in list(alive):
                        try:
                            next(gen)
                        except StopIteration:
                            alive.remove(gen)
                for i in pair:
                    scan(i)
            if d == 0:
                p.barrier()


ALL_PHASES = ("inproj0", "winattn", "mla", "out0", "mlp0", "inproj1", "gdnprep", "gla", "gdn", "out1", "mlp1")
_NC_CACHE = {}


def kernel(**inputs):
    x = np.asarray(inputs["x"], dtype=np.float32)
    B, S, Dm = x.shape
    if S not in _NC_CACHE:
        _NC_CACHE[S] = build(S, phases=ALL_PHASES)
    nc = _NC_CACHE[S]
    consts = host_consts(S)
    wts = {n: np.ascontiguousarray(np.asarray(inputs[n], dtype=np.float32)) for n in INPUT_SHAPES}
    in_maps = []
    for b in range(B):
        m = dict(wts)
        m.update(consts)
        m["xT"] = np.ascontiguousarray(x[b].T)
        in_maps.append(m)
    res = run_bass_kernel_spmd(nc, in_maps, core_ids=list(range(B)))
    out = np.stack([np.asarray(r["outT"]).T for r in res.results], 0)
    return np.ascontiguousarray(out, dtype=np.float32)
```
